# Optimizing a Trainium2 kernel written in Bass

```python
import math
import jax, jax.numpy as jnp
from jax import lax
import numpy as np

D_MODEL = 1024
BATCH = 4
SEQ = 4096
DEPTH = 2
DEC_BATCH = 8
DEC_SEQ = 8192
PAST_LEN = 128

D_MIX = D_MODEL
D_CONV = D_MIX // 2
D_RNN = D_MIX - D_CONV
D_IN = 2 * D_CONV + 2 * D_RNN
CONV_WIDTH = 31
RNN_HEADS = 8
RNN_HEAD_DIM = D_RNN // RNN_HEADS
RNN_CONV_WIDTH = 4
LRU_C = 8.0
N_DIR = 2
D_FF = 2816
FFN_CONV_WIDTH = 3
LN_EPS = 1e-5
DEEPNORM_ALPHA = (2.0 * DEPTH) ** 0.25
DEEPNORM_BETA = (8.0 * DEPTH) ** -0.25

kernel_name = "hymba_conformer_hawk_encoder"


def layer_norm(x, g, b):
    xf = x.astype(jnp.float32)
    mu = jnp.mean(xf, axis=-1, keepdims=True)
    var = jnp.mean(jnp.square(xf - mu), axis=-1, keepdims=True)
    y = (xf - mu) * lax.rsqrt(var + LN_EPS)
    return (y * g.astype(jnp.float32) + b.astype(jnp.float32)).astype(x.dtype)


def depthwise_conv(x, w, b, pad):
    y = lax.conv_general_dilated(
        x, w[:, None, :].astype(x.dtype), window_strides=(1,), padding=(pad,),
        dimension_numbers=("NWC", "WIO", "NWC"), feature_group_count=x.shape[-1])
    return y + b.astype(x.dtype)


def _lin_rec_combine(e1, e2):
    a1, b1 = e1
    a2, b2 = e2
    return (a1 * a2, a2 * b1 + b2)


def rg_lru(x, w_a, b_a, w_x, b_x, lam, reverse):
    bsz, seq, _ = x.shape
    xh = x.reshape(bsz, seq, RNN_HEADS, RNN_HEAD_DIM)
    gate_a = jnp.einsum("bshi,hij->bshj", xh, w_a).reshape(bsz, seq, D_RNN) + b_a
    gate_x = jnp.einsum("bshi,hij->bshj", xh, w_x).reshape(bsz, seq, D_RNN) + b_x
    r = jax.nn.sigmoid(gate_a.astype(jnp.float32))
    i = jax.nn.sigmoid(gate_x.astype(jnp.float32))
    log_a = -LRU_C * r * jax.nn.softplus(-lam.astype(jnp.float32))
    a = jnp.exp(log_a)
    b = jnp.sqrt(-jnp.expm1(2.0 * log_a)) * (i * x.astype(jnp.float32))
    _, h = lax.associative_scan(_lin_rec_combine, (a, b), axis=1, reverse=reverse)
    return h.astype(x.dtype)


def encoder_layer(x, w_in, conv_dw_w, conv_dw_b, conv_ln_g, conv_ln_b,
                  rnn_conv_w, rnn_conv_b, rg_w_a, rg_b_a, rg_w_x, rg_b_x, rg_lambda,
                  w_out, ln1_g, ln1_b, w_up, ffn_dw_w, ffn_dw_b, w_down, ln2_g, ln2_b):
    h = x @ w_in.astype(x.dtype)
    c_val, c_gate, r_x, r_gate = jnp.split(
        h, [D_CONV, 2 * D_CONV, 2 * D_CONV + D_RNN], axis=-1)
    c = c_val * jax.nn.sigmoid(c_gate)
    c = depthwise_conv(c, conv_dw_w, conv_dw_b, (CONV_WIDTH // 2, CONV_WIDTH // 2))
    c = jax.nn.silu(layer_norm(c, conv_ln_g, conv_ln_b))
    x_f = depthwise_conv(r_x, rnn_conv_w[0], rnn_conv_b[0], (RNN_CONV_WIDTH - 1, 0))
    x_b = depthwise_conv(r_x, rnn_conv_w[1], rnn_conv_b[1], (0, RNN_CONV_WIDTH - 1))
    rec = (rg_lru(x_f, rg_w_a[0], rg_b_a[0], rg_w_x[0], rg_b_x[0], rg_lambda[0], False)
           + rg_lru(x_b, rg_w_a[1], rg_b_a[1], rg_w_x[1], rg_b_x[1], rg_lambda[1], True))
    rec = rec * jax.nn.gelu(r_gate)
    mix = jnp.concatenate([c, rec], axis=-1) @ w_out.astype(x.dtype)
    x = layer_norm(DEEPNORM_ALPHA * x + mix, ln1_g, ln1_b)
    u = x @ w_up.astype(x.dtype)
    u = depthwise_conv(u, ffn_dw_w, ffn_dw_b, (FFN_CONV_WIDTH // 2, FFN_CONV_WIDTH // 2))
    val, gate = jnp.split(u, [D_FF], axis=-1)
    f = (jax.nn.gelu(gate) * val) @ w_down.astype(x.dtype)
    return layer_norm(DEEPNORM_ALPHA * x + f, ln2_g, ln2_b)


def encoder_trunk(x, ln_in_g, ln_in_b, w_in, conv_dw_w, conv_dw_b, conv_ln_g, conv_ln_b,
                  rnn_conv_w, rnn_conv_b, rg_w_a, rg_b_a, rg_w_x, rg_b_x, rg_lambda,
                  w_out, ln1_g, ln1_b, w_up, ffn_dw_w, ffn_dw_b, w_down, ln2_g, ln2_b):
    x = layer_norm(x, ln_in_g, ln_in_b)
    for l in range(DEPTH):
        x = encoder_layer(x, w_in[l], conv_dw_w[l], conv_dw_b[l], conv_ln_g[l], conv_ln_b[l],
                          rnn_conv_w[l], rnn_conv_b[l], rg_w_a[l], rg_b_a[l], rg_w_x[l],
                          rg_b_x[l], rg_lambda[l], w_out[l], ln1_g[l], ln1_b[l], w_up[l],
                          ffn_dw_w[l], ffn_dw_b[l], w_down[l], ln2_g[l], ln2_b[l])
    return x


def setup_inputs(seed: int = 0) -> dict:
    key = jax.random.key(seed)
    ks = jax.random.split(key, 26)
    f32 = jnp.float32

    def nrm(k, shape, scale):
        return jax.random.normal(k, shape, f32) * scale

    a0 = jax.random.uniform(ks[14], (DEPTH, N_DIR, D_RNN), f32, minval=0.9, maxval=0.999)
    s = a0 ** (1.0 / LRU_C)
    rg_lambda = jnp.log(s) - jnp.log1p(-s)
    return {
        "x_prompt": nrm(ks[0], (BATCH, SEQ, D_MODEL), 1.0),
        "x_sample": nrm(ks[1], (DEC_BATCH, DEC_SEQ, D_MODEL), 1.0),
        "ln_in_g": 1.0 + nrm(ks[2], (D_MODEL,), 0.02),
        "ln_in_b": nrm(ks[3], (D_MODEL,), 0.02),
        "w_in": nrm(ks[4], (DEPTH, D_MODEL, D_IN), D_MODEL ** -0.5),
        "conv_dw_w": nrm(ks[5], (DEPTH, CONV_WIDTH, D_CONV), CONV_WIDTH ** -0.5),
        "conv_dw_b": nrm(ks[6], (DEPTH, D_CONV), 0.02),
        "conv_ln_g": 1.0 + nrm(ks[7], (DEPTH, D_CONV), 0.02),
        "conv_ln_b": nrm(ks[8], (DEPTH, D_CONV), 0.02),
        "rnn_conv_w": nrm(ks[9], (DEPTH, N_DIR, RNN_CONV_WIDTH, D_RNN), RNN_CONV_WIDTH ** -0.5),
        "rnn_conv_b": nrm(ks[10], (DEPTH, N_DIR, D_RNN), 0.02),
        "rg_w_a": nrm(ks[11], (DEPTH, N_DIR, RNN_HEADS, RNN_HEAD_DIM, RNN_HEAD_DIM), RNN_HEAD_DIM ** -0.5),
        "rg_b_a": nrm(ks[12], (DEPTH, N_DIR, D_RNN), 0.02),
        "rg_w_x": nrm(ks[13], (DEPTH, N_DIR, RNN_HEADS, RNN_HEAD_DIM, RNN_HEAD_DIM), RNN_HEAD_DIM ** -0.5),
        "rg_b_x": nrm(ks[15], (DEPTH, N_DIR, D_RNN), 0.02),
        "rg_lambda": rg_lambda,
        "w_out": nrm(ks[16], (DEPTH, D_MIX, D_MODEL), DEEPNORM_BETA * D_MIX ** -0.5),
        "ln1_g": 1.0 + nrm(ks[17], (DEPTH, D_MODEL), 0.02),
        "ln1_b": nrm(ks[18], (DEPTH, D_MODEL), 0.02),
        "w_up": nrm(ks[19], (DEPTH, D_MODEL, 2 * D_FF), D_MODEL ** -0.5),
        "ffn_dw_w": nrm(ks[20], (DEPTH, FFN_CONV_WIDTH, 2 * D_FF), FFN_CONV_WIDTH ** -0.5),
        "ffn_dw_b": nrm(ks[21], (DEPTH, 2 * D_FF), 0.02),
        "w_down": nrm(ks[22], (DEPTH, D_FF, D_MODEL), DEEPNORM_BETA * D_FF ** -0.5),
        "ln2_g": 1.0 + nrm(ks[23], (DEPTH, D_MODEL), 0.02),
        "ln2_b": nrm(ks[24], (DEPTH, D_MODEL), 0.02),
    }


def reference(x_prompt, x_sample, ln_in_g, ln_in_b, w_in, conv_dw_w, conv_dw_b, conv_ln_g,
              conv_ln_b, rnn_conv_w, rnn_conv_b, rg_w_a, rg_b_a, rg_w_x, rg_b_x, rg_lambda,
              w_out, ln1_g, ln1_b, w_up, ffn_dw_w, ffn_dw_b, w_down, ln2_g, ln2_b):
    y_prompt = encoder_trunk(x_prompt, ln_in_g, ln_in_b, w_in, conv_dw_w, conv_dw_b, conv_ln_g,
                             conv_ln_b, rnn_conv_w, rnn_conv_b, rg_w_a, rg_b_a, rg_w_x, rg_b_x,
                             rg_lambda, w_out, ln1_g, ln1_b, w_up, ffn_dw_w, ffn_dw_b, w_down,
                             ln2_g, ln2_b)
    y_sample = encoder_trunk(x_sample, ln_in_g, ln_in_b, w_in, conv_dw_w, conv_dw_b, conv_ln_g,
                             conv_ln_b, rnn_conv_w, rnn_conv_b, rg_w_a, rg_b_a, rg_w_x, rg_b_x,
                             rg_lambda, w_out, ln1_g, ln1_b, w_up, ffn_dw_w, ffn_dw_b, w_down,
                             ln2_g, ln2_b)
    return (y_prompt, y_sample)
```

```python
import numpy as np
from contextlib import ExitStack
import concourse.bass as bass
import concourse.mybir as mybir
from concourse.bass_utils import run_bass_kernel_spmd

F32 = mybir.dt.float32
BF16 = mybir.dt.bfloat16
AF = mybir.ActivationFunctionType
ALU = mybir.AluOpType

N = 512
D = 1024
DC = 512
DFF = 2816
NUC = 44
NPC = 11
ALPHA = float((2.0 * 2) ** 0.25)
EPS = 1e-5
LRU_C = 8.0
ENGS = ("pe", "act", "dve", "pool", "sp")

UORD = []
for _j in range(NPC):
    UORD += [2 * _j, 2 * _j + 1, 22 + 2 * _j, 22 + 2 * _j + 1]


def _cvec_layout():
    off = {}
    cur = [0]

    def add(name, w):
        off[name] = cur[0]
        cur[0] += w

    add("lig", 8)
    add("lib", 8)
    for l in range(2):
        add(f"cdb{l}", 4)
        add(f"clg{l}", 4)
        add(f"clb{l}", 4)
        add(f"cdw{l}", 4 * 31)
        add(f"rcw{l}", 2 * 4 * 4)
        add(f"rcb{l}", 8)
        add(f"rba{l}", 8)
        add(f"rbx{l}", 8)
        add(f"lam{l}", 8)
        add(f"l1g{l}", 8)
        add(f"l1b{l}", 8)
        add(f"fdw{l}", NUC * 3)
        add(f"fdb{l}", NUC)
        add(f"l2g{l}", 8)
        add(f"l2b{l}", 8)
    return off, cur[0]


CV_OFF, CV_N = _cvec_layout()


def _pc(v):
    v = np.asarray(v, np.float32)
    return np.ascontiguousarray(v.reshape(-1, 128).T)


def _pack_cvec(inp):
    cv = np.zeros((128, CV_N), np.float32)

    def put(name, arr):
        arr = np.asarray(arr, np.float32).reshape(128, -1)
        cv[:, CV_OFF[name]:CV_OFF[name] + arr.shape[1]] = arr

    put("lig", _pc(inp["ln_in_g"]))
    put("lib", _pc(inp["ln_in_b"]))
    for l in range(2):
        put(f"cdb{l}", _pc(inp["conv_dw_b"][l]))
        put(f"clg{l}", _pc(inp["conv_ln_g"][l]))
        put(f"clb{l}", _pc(inp["conv_ln_b"][l]))
        w = np.asarray(inp["conv_dw_w"][l], np.float32)
        put(f"cdw{l}", w.reshape(31, 4, 128).transpose(2, 1, 0))
        w = np.asarray(inp["rnn_conv_w"][l], np.float32)
        put(f"rcw{l}", w.reshape(2, 4, 4, 128).transpose(3, 0, 2, 1))
        w = np.asarray(inp["rnn_conv_b"][l], np.float32)
        put(f"rcb{l}", w.reshape(2, 4, 128).transpose(2, 0, 1))
        put(f"rba{l}", np.asarray(inp["rg_b_a"][l], np.float32).reshape(2, 4, 128).transpose(2, 0, 1))
        put(f"rbx{l}", np.asarray(inp["rg_b_x"][l], np.float32).reshape(2, 4, 128).transpose(2, 0, 1))
        put(f"lam{l}", np.asarray(inp["rg_lambda"][l], np.float32).reshape(2, 4, 128).transpose(2, 0, 1))
        put(f"l1g{l}", _pc(inp["ln1_g"][l]))
        put(f"l1b{l}", _pc(inp["ln1_b"][l]))
        w = np.asarray(inp["ffn_dw_w"][l], np.float32).reshape(3, NUC, 128)[:, UORD, :]
        put(f"fdw{l}", w.transpose(2, 1, 0))
        w = np.asarray(inp["ffn_dw_b"][l], np.float32).reshape(NUC, 128)[UORD, :]
        put(f"fdb{l}", w.T)
        put(f"l2g{l}", _pc(inp["ln2_g"][l]))
        put(f"l2b{l}", _pc(inp["ln2_b"][l]))
    return cv


def _pack_weights(inp):
    out = {}
    w_in = np.asarray(inp["w_in"], np.float32)
    out["win_h"] = np.ascontiguousarray(
        w_in.reshape(2, 8, 128, 2048).transpose(0, 2, 1, 3)).reshape(2 * 128, 8 * 2048)
    w_out = np.asarray(inp["w_out"], np.float32)
    out["wout_h"] = np.ascontiguousarray(
        w_out.reshape(2, 8, 128, 1024).transpose(0, 2, 1, 3)).reshape(2 * 128, 8 * 1024)
    w_up = np.asarray(inp["w_up"], np.float32)
    wu = w_up.reshape(2, 8, 128, NUC, 128)[:, :, :, UORD, :]
    wu = wu.reshape(2, 8, 128, NPC, 512).transpose(0, 3, 2, 1, 4)
    out["wup_h"] = np.ascontiguousarray(wu).reshape(2 * NPC * 128, 8 * 512)
    w_dn = np.asarray(inp["w_down"], np.float32)
    wd = w_dn.reshape(2, 22, 128, 8, 128).transpose(0, 3, 2, 1, 4)
    out["wdn_h"] = np.ascontiguousarray(wd).reshape(2 * 8 * 128, 22 * 128)
    g = np.zeros((128, 32, 128), np.float32)
    for l in range(2):
        for d in range(2):
            for ti, nm in enumerate(("rg_w_a", "rg_w_x")):
                w = np.asarray(inp[nm][l][d], np.float32)
                for j in range(4):
                    idx = ((l * 2 + d) * 2 + ti) * 4 + j
                    g[0:64, idx, 0:64] = w[2 * j]
                    g[64:128, idx, 64:128] = w[2 * j + 1]
    out["gate_h"] = g.reshape(128, 32 * 128)
    out["ident"] = np.eye(128, dtype=np.float32)
    return out


class Op:
    __slots__ = ("eng", "fn", "deps", "signal", "ticket", "sem", "idx", "dma")

    def __init__(self, eng, fn):
        self.eng = eng
        self.fn = fn
        self.deps = []
        self.signal = False
        self.ticket = 0
        self.sem = None
        self.idx = 0
        self.dma = False


class Prog:
    def __init__(self, nc):
        self.nc = nc
        self.eng_sem = {}
        self.eng_cnt = {e: 0 for e in ENGS}
        self.eng_nops = {e: 0 for e in ENGS}
        self.dma_sems = {}
        self.dma_cnt = {}
        self.reset_phase()

    def reset_phase(self):
        self.ops = {e: [] for e in ENGS}
        self.last_w = {}
        self.readers = {}
        self.dma_last = {}

    def sem_for_engine(self, e):
        if e not in self.eng_sem:
            self.eng_sem[e] = self.nc.alloc_semaphore(name=f"sem_{e}")
        return self.eng_sem[e]

    def sem_for_dma(self, key):
        if key not in self.dma_sems:
            self.dma_sems[key] = self.nc.alloc_semaphore(name=f"dsem_{key}")
            self.dma_cnt[key] = 0
        return self.dma_sems[key]

    def _need(self, d, op):
        if d.dma or op.dma:
            return True
        if d.eng != op.eng:
            return True
        if op.eng == "pe":
            return False
        return (op.idx - d.idx) <= 3

    def add(self, eng, fn, reads=(), writes=(), dma_key=None):
        op = Op(eng, fn)
        op.idx = self.eng_nops[eng]
        self.eng_nops[eng] += 1
        cand = []
        for k in reads:
            w = self.last_w.get(k)
            if w is not None:
                cand.append(w)
        for k in writes:
            w = self.last_w.get(k)
            if w is not None:
                cand.append(w)
            cand.extend(self.readers.get(k, ()))
        if dma_key is not None:
            op.dma = True
            op.sem = self.sem_for_dma(dma_key)
            prev = self.dma_last.get(dma_key)
            if prev is not None:
                cand.append(prev)
            self.dma_cnt[dma_key] += 1
            op.ticket = 16 * self.dma_cnt[dma_key]
            op.signal = True
            self.dma_last[dma_key] = op
        else:
            op.sem = self.sem_for_engine(eng)
        best = {}
        for d in cand:
            if d is op or not self._need(d, op):
                continue
            key = ("d", id(d.sem)) if d.dma else ("e", d.eng)
            b = best.get(key)
            if b is None or (d.ticket > b.ticket if d.dma else d.idx > b.idx):
                best[key] = d
        op.deps = list(best.values())
        for d in op.deps:
            d.signal = True
        for k in reads:
            lst = self.readers.setdefault(k, [])
            if not op.dma:
                lst[:] = [r for r in lst if r.dma or r.eng != eng]
            lst.append(op)
        for k in writes:
            self.last_w[k] = op
            self.readers[k] = []
        self.ops[eng].append(op)
        return op

    def emit_phase(self, name="phase"):
        nc = self.nc
        for e in ENGS:
            for op in self.ops[e]:
                if not op.dma and op.signal:
                    self.eng_cnt[e] += 1
                    op.ticket = self.eng_cnt[e]
        ops = self.ops

        def run(ename, eng):
            waited = {}
            for op in ops[ename]:
                for d in op.deps:
                    sid = id(d.sem)
                    if waited.get(sid, 0) < d.ticket:
                        eng.wait_ge(d.sem, d.ticket)
                        waited[sid] = d.ticket
                ins = op.fn(eng)
                if op.signal:
                    ins.then_inc(op.sem, 16 if op.dma else 1)
            if ename == "sp":
                for k, sem in self.dma_sems.items():
                    v = 16 * self.dma_cnt[k]
                    if v > 0 and waited.get(id(sem), 0) < v:
                        eng.wait_ge(sem, v)

        with nc.named_scope(name), nc.Block() as blk:
            blk.tensor(lambda e: run("pe", e))
            blk.scalar(lambda e: run("act", e))
            blk.vector(lambda e: run("dve", e))
            blk.gpsimd(lambda e: run("pool", e))
            blk.sync(lambda e: run("sp", e))
        self.reset_phase()

    def emit_final_fence(self):
        nc = self.nc
        sems = [(self.dma_sems[k], 16 * self.dma_cnt[k]) for k in self.dma_sems]

        with nc.Block() as blk:
            def f(e):
                for s, v in sems:
                    if v > 0:
                        e.wait_ge(s, v)
            blk.sync(f)


class Builder:
    def __init__(self, nc, seq_lens):
        self.nc = nc
        self.P = Prog(nc)
        self.seq_lens = seq_lens
        self.rot = {}

    def un(self, name):
        self.uid = getattr(self, "uid", 0) + 1
        return f"{name}_u{self.uid}"

    def nxt(self, name, n):
        v = self.rot.get(name, 0)
        self.rot[name] = v + 1
        return v % n

    def pe(self, fn, r=(), w=()):
        return self.P.add("pe", fn, r, w)

    def act(self, fn, r=(), w=()):
        return self.P.add("act", fn, r, w)

    def dve(self, fn, r=(), w=()):
        return self.P.add("dve", fn, r, w)

    def pool(self, fn, r=(), w=()):
        return self.P.add("pool", fn, r, w)

    def dma(self, out, in_, key, r=(), w=(), slow=False, q="sp"):
        if slow:
            fn = lambda e: e.dma_start(out=out, in_=in_, allow_slow_non_contiguous=True)
        else:
            fn = lambda e: e.dma_start(out=out, in_=in_)
        return self.P.add(q, fn, r, w, dma_key=key)

    def cv(self, name, col, width=1):
        o = CV_OFF[name] + col
        return self.cvec[:, o:o + width]

    def declare(self):
        nc = self.nc
        dt = nc.dram_tensor
        self.x_in = []
        self.y_out = []
        for si, S in enumerate(self.seq_lens):
            self.x_in.append(dt(f"x{si}", [S, D], F32, kind="ExternalInput").ap())
            self.y_out.append(dt(f"y{si}", [S, D], F32, kind="ExternalOutput").ap())
        self.cvec_h = dt("cvec", [128, CV_N], F32, kind="ExternalInput").ap()
        self.win_h = dt("win_h", [256, 8 * 2048], F32, kind="ExternalInput").ap()
        self.wout_h = dt("wout_h", [256, 8 * 1024], F32, kind="ExternalInput").ap()
        self.wup_h = dt("wup_h", [2 * NPC * 128, 4096], F32, kind="ExternalInput").ap()
        self.wdn_h = dt("wdn_h", [2 * 8 * 128, 2816], F32, kind="ExternalInput").ap()
        self.gate_h = dt("gate_h", [128, 4096], F32, kind="ExternalInput").ap()
        self.ident_h = dt("ident", [128, 128], F32, kind="ExternalInput").ap()
        self.win_b = dt("win_b", [256, 8 * 2048], BF16, kind="Internal").ap()
        self.wout_b = dt("wout_b", [256, 8 * 1024], BF16, kind="Internal").ap()
        self.wup_b = dt("wup_b", [2 * NPC * 128, 4096], BF16, kind="Internal").ap()
        self.wdn_b = dt("wdn_b", [2 * 8 * 128, 2816], BF16, kind="Internal").ap()
        self.XF = []
        self.XB = []
        self.X1F = []
        self.X1B = []
        self.CS = []
        self.CN = []
        self.HB = []
        for si, S in enumerate(self.seq_lens):
            self.XF.append([dt(f"xf{si}_{j}", [D, S + 1], F32, kind="Internal").ap() for j in range(3)])
            self.XB.append([dt(f"xb{si}_{j}", [D, S + 1], BF16, kind="Internal").ap() for j in range(2)])
            self.X1F.append(dt(f"x1f{si}", [D, S + 1], F32, kind="Internal").ap())
            self.X1B.append(dt(f"x1b{si}", [D, S], BF16, kind="Internal").ap())
            self.CS.append(dt(f"cs{si}", [DC, S + 30], BF16, kind="Internal").ap())
            self.CN.append(dt(f"cn{si}", [DC, S], BF16, kind="Internal").ap())
            self.HB.append(dt(f"hb{si}", [DC, S], F32, kind="Internal").ap())

    def build(self):
        nc = self.nc
        self.declare()
        with ExitStack() as gs:
            sb = lambda name, shape, dtp: gs.enter_context(nc.sbuf_tensor(name, shape, dtp))
            self.cvec = sb("cvec_sb", [128, CV_N], F32)
            self.lamc = sb("lamc", [128, 16], F32)
            self.ident = sb("ident_sb", [128, 128], F32)
            self.ones512 = sb("ones512", [128, 128], BF16)
            self.ones1024 = sb("ones1024", [128, 128], BF16)
            self.gates = sb("gates_sb", [128, 32, 128], BF16)
            self.zeros = sb("zeros_sb", [128, 64], F32)
            self.hbias = sb("hbias_sb", [128, 32], F32)
            self.lamh = sb("lamh_sb", [128, 16], F32)
            self.qc = sb("qc_sb", [128, 1], F32)
            self.zerosb = sb("zerosb_sb", [128, 64], BF16)
            self.phase_weights()
            for si in range(len(self.seq_lens)):
                self.phase_p0(si)
                for l in range(2):
                    self.phase_s1(si, l)
                    self.phase_s2a(si, l)
                    self.phase_s2b(si, l)
                self.phase_out(si)
            self.P.emit_final_fence()

    def psum_banks(self, es, n=8):
        return [es.enter_context(self.nc.psum_tensor(self.un(f"bank{i}"), [128, 512], F32)) for i in range(n)]

    def phase_weights(self):
        nc = self.nc
        with ExitStack() as es:
            sb = lambda name, shape, dtp: es.enter_context(nc.sbuf_tensor(self.un(name), shape, dtp))
            fin = [sb(f"fin{i}", [128, 2048], F32) for i in range(4)]
            fout = [sb(f"fout{i}", [128, 2048], BF16) for i in range(4)]
            tmp = sb("wtmp", [128, 16], F32)
            self.dma(self.cvec[:], self.cvec_h, "c0", w=["cvec"])
            self.dma(self.ident[:], self.ident_h, "c1", w=["ident"])
            self.dve(lambda e: e.memset(self.ones512[:], 1.0 / 512.0), w=["ones"])
            self.dve(lambda e: e.memset(self.ones1024[:], 1.0 / 1024.0), w=["ones"])
            self.dve(lambda e: e.memset(self.zeros[:], 0.0), w=["zeros"])
            self.dve(lambda e: e.memset(self.zerosb[:], 0.0), w=["zeros"])
            for l in range(2):
                lam = self.cv(f"lam{l}", 0, 8)
                dst = self.lamc[:, 8 * l:8 * l + 8]
                t = tmp[:, 8 * l:8 * l + 8]
                self.act(lambda e, t=t, lam=lam: e.activation(out=t, in_=lam, func=AF.Exp, scale=-1.0),
                         r=["cvec"], w=[("wtmp", l)])
                self.act(lambda e, t=t: e.activation(out=t, in_=t, func=AF.Ln, bias=1.0),
                         r=[("wtmp", l)], w=[("wtmp", l)])
                self.dve(lambda e, t=t, dst=dst: e.tensor_scalar(out=dst, in0=t, scalar1=-LRU_C, scalar2=None,
                                                                 op0=ALU.mult),
                         r=[("wtmp", l)], w=["lamc"])
            self.dve(lambda e: e.tensor_scalar(out=self.lamh[:], in0=self.lamc[:], scalar1=0.5, scalar2=None,
                                               op0=ALU.mult), r=["lamc"], w=["lamh"])
            self.dve(lambda e: e.memset(self.qc[:], 0.25), w=["qc"])
            for l in range(2):
                for ti, nm in enumerate(("rba", "rbx")):
                    dst = self.hbias[:, (l * 2 + ti) * 8:(l * 2 + ti) * 8 + 8]
                    srcc = self.cv(f"{nm}{l}", 0, 8)
                    self.dve(lambda e, dst=dst, srcc=srcc: e.tensor_scalar(out=dst, in0=srcc, scalar1=0.5, scalar2=None,
                                                                           op0=ALU.mult), r=["cvec"], w=["hbias"])
            jobs = []
            for src, dst in ((self.win_h, self.win_b), (self.wout_h, self.wout_b),
                             (self.wup_h, self.wup_b), (self.wdn_h, self.wdn_b)):
                R, C = src.shape
                for r0 in range(0, R, 128):
                    for c0 in range(0, C, 2048):
                        c1 = min(C, c0 + 2048)
                        jobs.append((src[r0:r0 + 128, c0:c1], dst[r0:r0 + 128, c0:c1], c1 - c0))
            gflat = self.gates[:].rearrange("p a b -> p (a b)")
            for c0 in range(0, 4096, 2048):
                jobs.append((self.gate_h[:, c0:c0 + 2048], gflat[:, c0:c0 + 2048], -2048))
            LOOK = 3
            for i in range(min(LOOK, len(jobs))):
                self.dma(fin[i % 4][:, :abs(jobs[i][2])], jobs[i][0], f"fin{i % 4}", w=[("fin", i % 4)])
            for i, (src, dst, w) in enumerate(jobs):
                s = i % 4
                to_sbuf = w < 0
                w = abs(w)
                o = dst if to_sbuf else fout[s][:, :w]
                wk = ["gates"] if to_sbuf else [("fout", s)]
                which = i % 3
                if which == 0:
                    self.act(lambda e, o=o, s=s, w=w: e.activation(out=o, in_=fin[s][:, :w], func=AF.Copy),
                             r=[("fin", s)], w=wk)
                elif which == 1:
                    self.dve(lambda e, o=o, s=s, w=w: e.tensor_copy(out=o, in_=fin[s][:, :w]),
                             r=[("fin", s)], w=wk)
                else:
                    self.pool(lambda e, o=o, s=s, w=w: e.tensor_copy(out=o, in_=fin[s][:, :w]),
                              r=[("fin", s)], w=wk)
                if not to_sbuf:
                    self.dma(dst, fout[s][:, :w], f"fout{s}", r=[("fout", s)])
                if i + LOOK < len(jobs):
                    i2 = i + LOOK
                    self.dma(fin[i2 % 4][:, :abs(jobs[i2][2])], jobs[i2][0], f"fin{i2 % 4}", w=[("fin", i2 % 4)])
            for si, S in enumerate(self.seq_lens):
                cs = self.CS[si].rearrange("(k p) s -> p k s", p=128)
                self.dma(cs[:, :, 0:15], self.zerosb[:, 0:60].rearrange("p (k s) -> p k s", k=4), "z0",
                         r=["zeros"], slow=True)
                self.dma(cs[:, :, S + 15:S + 30], self.zerosb[:, 0:60].rearrange("p (k s) -> p k s", k=4), "z1",
                         r=["zeros"], slow=True)
                x1 = self.X1F[si].rearrange("(k p) s -> p k s", p=128)
                self.dma(x1[:, :, 0:1], self.zeros[:, 0:8].rearrange("p (k s) -> p k s", k=8), "z2",
                         r=["zeros"], slow=True)
            self.P.emit_phase("W")

    def ln_alloc(self, es, tag):
        nc = self.nc
        sb = lambda name, shape, dtp: es.enter_context(nc.sbuf_tensor(self.un(f"{tag}_{name}"), shape, dtp))
        L = {}
        L["ybf"] = [sb(f"ybf{i}", [128, N], BF16) for i in range(3)]
        L["ysq"] = [sb(f"ysq{i}", [128, N], BF16) for i in range(3)]
        L["mean"] = sb("mean", [128, N], F32)
        L["m2"] = sb("m2", [128, N], F32)
        L["var"] = L["m2"]
        L["rstd"] = sb("rstd", [128, N], F32)
        L["t"] = [sb(f"t{i}", [128, N], F32) for i in range(2)]
        return L

    def emit_ln(self, *args, **kw):
        for _ in self.emit_ln_gen(*args, **kw):
            pass

    def ln_stats_act(self, L, y, key, n, nslots=2, pool_copy=False):
        s = self.nxt("lnst", nslots)
        ybf = L["ybf"][s]
        ysq = L["ysq"][s]
        if pool_copy:
            self.pool(lambda e: e.tensor_copy(out=ybf[:, :n], in_=y), r=[key], w=[("ybf", s)])
        else:
            self.act(lambda e: e.activation(out=ybf[:, :n], in_=y, func=AF.Copy), r=[key], w=[("ybf", s)])
        self.act(lambda e: e.activation(out=ysq[:, :n], in_=y, func=AF.Square), r=[key], w=[("ysq", s)])
        return s

    def ln_stats_pe(self, L, s, k, nch, n, C, bank_m, bank_q):
        ones = self.ones512 if C == 512 else self.ones1024
        bm, km = bank_m
        bq, kq = bank_q
        ybf = L["ybf"][s]
        ysq = L["ysq"][s]
        self.pe(lambda e: e.matmul(bm[:, :n], lhsT=ones[:], rhs=ybf[:, :n], start=(k == 0), stop=(k == nch - 1)),
                r=[("ybf", s), "ones"], w=[km])
        self.pe(lambda e: e.matmul(bq[:, :n], lhsT=ones[:], rhs=ysq[:, :n], start=(k == 0), stop=(k == nch - 1)),
                r=[("ysq", s), "ones"], w=[kq])

    def ln_stats_chunk(self, L, y, key, k, nch, n, C, bank_m, bank_q):
        s = self.ln_stats_act(L, y, key, n)
        self.ln_stats_pe(L, s, k, nch, n, C, bank_m, bank_q)

    def ln_finalize(self, L, n, bank_m, bank_q):
        bm, km = bank_m
        bq, kq = bank_q
        mean, m2, var, rstd = L["mean"], L["m2"], L["var"], L["rstd"]
        self.act(lambda e: e.activation(out=mean[:, :n], in_=bm[:, :n], func=AF.Copy), r=[km], w=["ln_mean"])
        self.act(lambda e: e.activation(out=m2[:, :n], in_=bm[:, :n], func=AF.Square), r=[km], w=["ln_m2"])
        self.dve(lambda e: e.tensor_tensor(out=var[:, :n], in0=bq[:, :n], in1=m2[:, :n], op=ALU.subtract),
                 r=[kq, "ln_m2"], w=["ln_m2"])
        self.act(lambda e: e.activation(out=var[:, :n], in_=var[:, :n], func=AF.Sqrt, bias=self.epsc[:, 0:1]),
                 r=["ln_m2"], w=["ln_m2"])
        self.dve(lambda e: e.reciprocal(out=rstd[:, :n], in_=var[:, :n]), r=["ln_m2"], w=["ln_rstd"])

    def ln_norm_chunk(self, L, y, key, k, n, gname, bname, func, outs_k, pool_only=False, pool_dup=False):
        mean, rstd = L["mean"], L["rstd"]
        s = self.nxt("lnt", 2)
        t = L["t"][s]
        first = self.pool if (k % 2 == 0 or pool_only) else self.dve
        second = self.pool if pool_only else self.dve
        first(lambda e: e.tensor_tensor(out=t[:, :n], in0=y, in1=mean[:, :n], op=ALU.subtract),
              r=[key, "ln_mean"], w=[("lnt", s)])
        second(lambda e: e.tensor_tensor(out=t[:, :n], in0=t[:, :n], in1=rstd[:, :n], op=ALU.mult),
               r=[("lnt", s), "ln_rstd"], w=[("lnt", s)])
        g = self.cv(gname, k)
        b = self.cv(bname, k)
        if pool_dup and len(outs_k) == 2:
            (o0, k0), (o1, k1) = outs_k
            self.act(lambda e: e.activation(out=o0, in_=t[:, :n], func=func, bias=b, scale=g),
                     r=[("lnt", s), "cvec"], w=[k0])
            self.pool(lambda e: e.tensor_copy(out=o1, in_=o0), r=[k0], w=[k1])
            return
        for (o, okey) in outs_k:
            self.act(lambda e, o=o: e.activation(out=o, in_=t[:, :n], func=func, bias=b, scale=g),
                     r=[("lnt", s), "cvec"], w=[okey])

    def emit_ln_gen(self, L, ys, n, C, gname, bname, func, outs, bank_m, bank_q, pool_help=False):
        nch = len(ys)
        for k, (y, key) in enumerate(ys):
            s_ = self.ln_stats_act(L, y, key, n, 2, pool_copy=pool_help)
            self.ln_stats_pe(L, s_, k, nch, n, C, bank_m, bank_q)
        yield
        self.ln_finalize(L, n, bank_m, bank_q)
        yield
        for k, (y, key) in enumerate(ys):
            self.ln_norm_chunk(L, y, key, k, n, gname, bname, func, outs[k], pool_dup=pool_help)
            if k % 4 == 3 and k != nch - 1:
                yield

    def phase_p0(self, si):
        nc = self.nc
        S = self.seq_lens[si]
        T = S // N
        x = self.x_in[si]
        XF = self.XF[si][0].rearrange("(k p) s -> p k s", p=128)
        XB = self.XB[si][0].rearrange("(k p) s -> p k s", p=128)
        with ExitStack() as es:
            sb = lambda name, shape, dtp: es.enter_context(nc.sbuf_tensor(self.un(name), shape, dtp))
            self.epsc = sb("epsc", [128, 1], F32)
            self.dve(lambda e: e.memset(self.epsc[:], EPS), w=["epsc"])
            xin = [sb(f"xin{i}", [128, D], F32) for i in range(3)]
            R = [sb(f"R{i}", [128, 8, N], F32) for i in range(2)]
            RB = [sb(f"RB{i}", [128, 8, N], BF16) for i in range(2)]
            L = self.ln_alloc(es, "p0")
            pt = [es.enter_context(nc.psum_tensor(self.un(f"pt{i}"), [128, 1024], F32)) for i in range(2)]
            bm = es.enter_context(nc.psum_tensor(self.un("bm"), [128, 512], F32))
            bq = es.enter_context(nc.psum_tensor(self.un("bq"), [128, 512], F32))
            def stage_t(i):
                rs = i % 2
                for tb in range(4):
                    xs = self.nxt("xin", 3)
                    ps = self.nxt("pt", 2)
                    row0 = i * N + tb * 128
                    self.dma(xin[xs][:], x[row0:row0 + 128, :], f"xin{xs}", w=[("xin", xs)])
                    for k in range(8):
                        self.pe(lambda e, xs=xs, ps=ps, k=k: e.transpose(
                            out=pt[ps][:, k * 128:(k + 1) * 128], in_=xin[xs][:, k * 128:(k + 1) * 128],
                            identity=self.ident[:]),
                            r=[("xin", xs), "ident"], w=[("pt", ps)])
                    src_v = pt[ps][:].rearrange("p (k c) -> p k c", k=8)
                    dst_v = R[rs][:, :, tb * 128:(tb + 1) * 128]
                    if tb % 2 == 0:
                        self.act(lambda e, src_v=src_v, dst_v=dst_v: e.activation(out=dst_v, in_=src_v, func=AF.Copy),
                                 r=[("pt", ps)], w=[("R", rs, k) for k in range(8)])
                    else:
                        self.dve(lambda e, src_v=src_v, dst_v=dst_v: e.tensor_copy(out=dst_v, in_=src_v),
                                 r=[("pt", ps)], w=[("R", rs, k) for k in range(8)])
                    yield

            def stage_l(i):
                rs = i % 2
                ys = [(R[rs][:, k, :], ("R", rs, k)) for k in range(8)]
                outs = [[(R[rs][:, k, :], ("R", rs, k)), (RB[rs][:, k, :], ("RB", rs, k))] for k in range(8)]
                for _ in self.emit_ln_gen(L, ys, N, 1024, "lig", "lib", AF.Identity, outs, (bm, "bm"), (bq, "bq"),
                                          pool_help=False):
                    yield
                c0 = 1 + i * N
                self.dma(XF[:, :, c0:c0 + N], R[rs][:], f"R{rs}", r=[("R", rs, k) for k in range(8)], q="act")
                self.dma(XB[:, :, c0:c0 + N], RB[rs][:], f"RB{rs}", r=[("RB", rs, k) for k in range(8)], q="act")
                yield

            prev = None
            for i in range(T):
                for _ in stage_t(i):
                    if prev is not None:
                        next(prev, None)
                if prev is not None:
                    for _ in prev:
                        pass
                prev = stage_l(i)
            for _ in prev:
                pass
            self.P.emit_phase(f"P0_{si}")

    def rnn_alloc(self, es):
        nc = self.nc
        sb = lambda name, shape, dtp: es.enter_context(nc.sbuf_tensor(self.un(name), shape, dtp))
        Rn = {}
        Rn["rbuf"] = [sb(f"rbuf{i}", [128, 4, N + 3], F32) for i in range(2)]
        Rn["xcv"] = [sb(f"xcv{i}", [128, N], F32) for i in range(3)]
        Rn["xcb"] = [sb(f"xcb{i}", [128, N], BF16) for i in range(2)]
        Rn["rr"] = [sb(f"rr{i}", [128, N], F32) for i in range(2)]
        Rn["ig"] = [sb(f"ig{i}", [128, N], F32) for i in range(2)]
        Rn["aa"] = [sb(f"aa{i}", [128, N], F32) for i in range(2)]
        Rn["sq"] = [sb(f"sq{i}", [128, N], F32) for i in range(2)]
        Rn["bb"] = [sb(f"bb{i}", [128, N], F32) for i in range(2)]
        Rn["h"] = [sb(f"hs{i}", [128, 4, N], F32) for i in range(2)]
        return Rn

    def emit_rnn(self, *a, **k):
        for _ in self.emit_rnn_gen(*a, **k):
            pass

    def emit_rnn_gen(self, Rn, l, d, i, first, rx_src, g_banks, reverse):
        s = i % 2
        o = 1 - s
        rbuf = Rn["rbuf"][s]
        rprev = Rn["rbuf"][o]
        off = 0 if reverse else 3
        for j in range(4):
            bank, bkey = rx_src(j)
            self.dve(lambda e, j=j, bank=bank: e.tensor_copy(out=rbuf[:, j, off:off + N], in_=bank[:, :]),
                     r=[bkey], w=[("rbuf", s, j)])
            if reverse:
                dst = rbuf[:, j, N:N + 3]
                src = rprev[:, j, 0:3]
            else:
                dst = rbuf[:, j, 0:3]
                src = rprev[:, j, N:N + 3]
            if first:
                self.pool(lambda e, dst=dst: e.memset(dst, 0.0), w=[("rbuf", s, j)])
            else:
                self.pool(lambda e, dst=dst, src=src: e.tensor_copy(out=dst, in_=src),
                          r=[("rbuf", o, j)], w=[("rbuf", s, j)])
            yield
        hs = Rn["h"][s]
        hprev = Rn["h"][o]

        def part_x(j):
            q = j % 2
            q3 = j % 3
            xcv = Rn["xcv"][q3]
            xcb = Rn["xcb"][q]
            wof = (d * 4 + j) * 4
            bcol = self.cv(f"rcb{l}", d * 4 + j)
            w0 = self.cv(f"rcw{l}", wof + 0)
            self.dve(lambda e: e.tensor_scalar(
                out=xcv[:], in0=rbuf[:, j, 0:N], scalar1=w0, scalar2=bcol, op0=ALU.mult, op1=ALU.add),
                r=[("rbuf", s, j), "cvec"], w=[("xcv", q3)])
            for k in range(1, 4):
                wk = self.cv(f"rcw{l}", wof + k)
                self.dve(lambda e, k=k, wk=wk: e.scalar_tensor_tensor(
                    out=xcv[:], in0=rbuf[:, j, k:k + N], scalar=wk, in1=xcv[:], op0=ALU.mult, op1=ALU.add),
                    r=[("rbuf", s, j), ("xcv", q3)], w=[("xcv", q3)])
            self.pool(lambda e: e.tensor_copy(out=xcb[:], in_=xcv[:]), r=[("xcv", q3)], w=[("xcb", q)])

        def part_x1b(j):
            q = j % 2
            xcb, rr, ig, aa = (Rn[nm][q] for nm in ("xcb", "rr", "ig", "aa"))
            (ga, gak), (gx, gxk) = g_banks[self.nxt("gb", len(g_banks))]
            ia = ((l * 2 + d) * 2 + 0) * 4 + j
            ix = ((l * 2 + d) * 2 + 1) * 4 + j
            self.pe(lambda e: e.matmul(ga[:, :], lhsT=self.gates[:, ia, :], rhs=xcb[:], start=True, stop=True),
                    r=[("xcb", q), "gates"], w=[gak])
            self.pe(lambda e: e.matmul(gx[:, :], lhsT=self.gates[:, ix, :], rhs=xcb[:], start=True, stop=True),
                    r=[("xcb", q), "gates"], w=[gxk])
            cb = d * 4 + j
            ba = self.hbias[:, (l * 2 + 0) * 8 + cb:(l * 2 + 0) * 8 + cb + 1]
            bx = self.hbias[:, (l * 2 + 1) * 8 + cb:(l * 2 + 1) * 8 + cb + 1]
            lh = self.lamh[:, 8 * l + cb:8 * l + cb + 1]
            self.act(lambda e: e.activation(out=rr[:], in_=ga[:, :], func=AF.Tanh, bias=ba, scale=0.5),
                     r=[gak, "hbias"], w=[("rr", q)])
            self.act(lambda e: e.activation(out=ig[:], in_=gx[:, :], func=AF.Tanh, bias=bx, scale=0.5),
                     r=[gxk, "hbias"], w=[("ig", q)])
            self.act(lambda e: e.activation(out=aa[:], in_=rr[:], func=AF.Exp, bias=lh, scale=lh),
                     r=[("rr", q), "lamh"], w=[("aa", q)])

        def part_x2(j):
            q = j % 2
            aa, sq = (Rn[nm][q] for nm in ("aa", "sq"))
            self.act(lambda e: e.activation(out=sq[:], in_=aa[:], func=AF.Square),
                     r=[("aa", q)], w=[("sq", q)])
            self.act(lambda e: e.activation(out=sq[:], in_=sq[:], func=AF.Sqrt, bias=self.qc[:, 0:1], scale=-0.25),
                     r=[("sq", q)], w=[("sq", q)])


        def part_y(j):
            q = j % 2
            aa, sq, bb, ig = (Rn[nm][q] for nm in ("aa", "sq", "bb", "ig"))
            xcv = Rn["xcv"][j % 3]
            self.dve(lambda e: e.scalar_tensor_tensor(out=bb[:], in0=ig[:], scalar=1.0, in1=xcv[:], op0=ALU.add,
                                                      op1=ALU.mult),
                     r=[("ig", q), ("xcv", j % 3)], w=[("bb", q)])
            self.dve(lambda e: e.tensor_tensor(out=bb[:], in0=bb[:], in1=sq[:], op=ALU.mult),
                     r=[("bb", q), ("sq", q)], w=[("bb", q)])
            if reverse:
                init = 0.0 if first else hprev[:, j, 0:1]
                self.dve(lambda e: e.tensor_tensor_scan(
                    out=hs[:, j, ::-1], data0=aa[:, ::-1], data1=bb[:, ::-1], initial=init,
                    op0=ALU.mult, op1=ALU.add),
                    r=[("aa", q), ("bb", q), ("h", o, j)], w=[("h", s, j)])
            else:
                init = 0.0 if first else hprev[:, j, N - 1:N]
                self.dve(lambda e: e.tensor_tensor_scan(
                    out=hs[:, j, :], data0=aa[:], data1=bb[:], initial=init, op0=ALU.mult, op1=ALU.add),
                    r=[("aa", q), ("bb", q), ("h", o, j)], w=[("h", s, j)])

        for st in range(7):
            if 3 <= st:
                part_y(st - 3)
            if 2 <= st <= 5:
                part_x2(st - 2)
            if 1 <= st <= 4:
                part_x1b(st - 1)
            if st < 4:
                part_x(st)
            yield

    def phase_s1(self, si, l):
        nc = self.nc
        S = self.seq_lens[si]
        T = S // N
        XB = self.XB[si][l].rearrange("(k p) s -> p k s", p=128)
        CS = self.CS[si].rearrange("(k p) s -> p k s", p=128)
        CN = self.CN[si].rearrange("(k p) s -> p k s", p=128)
        HB = self.HB[si].rearrange("(k p) s -> p k s", p=128)
        with ExitStack() as es:
            sb = lambda name, shape, dtp: es.enter_context(nc.sbuf_tensor(self.un(name), shape, dtp))
            self.epsc = sb("epsc", [128, 1], F32)
            self.onec = sb("onec", [128, 1], F32)
            self.dve(lambda e: e.memset(self.epsc[:], EPS), w=["epsc"])
            self.dve(lambda e: e.memset(self.onec[:], 1.0), w=["onec"])
            win = sb("win", [128, 8, 1536], BF16)
            dg = sb("dg", [128, 4, 31, 128], BF16)
            xbt = [sb(f"xbt{i}", [128, 8, N], BF16) for i in range(2)]
            cst = [sb(f"cst{i}", [128, 4, N], BF16) for i in range(2)]
            sg = [sb(f"sg{i}", [128, N], F32) for i in range(2)]
            cin = [sb(f"cin{i}", [128, 4, N + 30], BF16) for i in range(2)]
            cc = sb("cc", [128, 4, N], F32)
            cn = [sb(f"cn{i}", [128, 4, N], BF16) for i in range(2)]
            Rn = self.rnn_alloc(es)
            L = self.ln_alloc(es, "s1")
            banks = self.psum_banks(es)
            wsrc = self.win_b[l * 128:(l + 1) * 128, :].rearrange("p (k m) -> p k m", k=8)
            for h in range(2):
                self.dma(win[:, 4 * h:4 * h + 4, :], wsrc[:, 4 * h:4 * h + 4, 0:1536], f"w{h}", w=[("win", h)])
            wkeys = [("win", 0), ("win", 1)]
            for j in range(4):
                for k in range(31):
                    wcol = self.cv(f"cdw{l}", j * 31 + k)
                    self.dve(lambda e, j=j, k=k, wcol=wcol: e.tensor_scalar(
                        out=dg[:, j, k, :], in0=self.ident[:], scalar1=wcol, scalar2=None, op0=ALU.mult),
                        r=["ident", "cvec"], w=[("dg", j)])

            def conv_gen(ti):
                cs_ = self.nxt("cin", 2)
                ci = cin[cs_]
                c0 = ti * N
                self.dma(ci[:], CS[:, :, c0:c0 + N + 30], f"cin{cs_}",
                         r=[("CS", ti - 1), ("CS", ti), ("CS", ti + 1)], w=[("cin", cs_)])
                yield
                bkm = (banks[0], ("bank", 0))
                bkq = (banks[1], ("bank", 1))
                for j in range(4):
                    for k in range(31):
                        self.pe(lambda e, j=j, k=k: e.matmul(banks[6][:, :], lhsT=dg[:, j, k, :], rhs=ci[:, j, k:k + N],
                                                             start=(k == 0), stop=(k == 30)),
                                r=[("cin", cs_), ("dg", j)], w=[("bank", 6)])
                    bcol = self.cv(f"cdb{l}", j)
                    self.act(lambda e, j=j, bcol=bcol: e.activation(out=cc[:, j, :], in_=banks[6][:, :],
                                                                    func=AF.Identity, bias=bcol),
                             r=[("bank", 6), "cvec"], w=[("cc", j)])
                    yield
                for j in range(4):
                    self.ln_stats_chunk(L, cc[:, j, :], ("cc", j), j, 4, N, 512, bkm, bkq)
                self.ln_finalize(L, N, bkm, bkq)
                ns = self.nxt("cn", 2)
                for j in range(4):
                    self.ln_norm_chunk(L, cc[:, j, :], ("cc", j), j, N, f"clg{l}", f"clb{l}", AF.Silu,
                                       [(cn[ns][:, j, :], ("cn", ns, j))])
                self.dma(CN[:, :, c0:c0 + N], cn[ns][:], f"cn{ns}", r=[("cn", ns, j) for j in range(4)], q="act")
                yield

            def mm_tile(slot, bank, bkey, m):
                for k in range(8):
                    self.pe(lambda e, k=k: e.matmul(
                        bank[:, :], lhsT=win[:, k, m * 128:(m + 1) * 128], rhs=xbt[slot][:, k, :],
                        start=(k == 0), stop=(k == 7)),
                        r=[("xbt", slot)] + wkeys, w=[bkey])

            def load_x(it):
                i = T - 1 - it
                s = it % 2
                c0 = i * N
                self.dma(xbt[s][:], XB[:, :, 1 + c0:1 + c0 + N], f"xbt{s}", w=[("xbt", s)])

            def glu_gen(it):
                i = T - 1 - it
                s = it % 2
                c0 = i * N
                for j in range(4):
                    bv, bg = banks[0], banks[1]
                    kv, kg = ("bank", 0), ("bank", 1)
                    mm_tile(s, bg, kg, 4 + j)
                    mm_tile(s, bv, kv, j)
                    q = self.nxt("sg", 2)
                    self.act(lambda e, q=q: e.activation(out=sg[q][:], in_=bg[:, :], func=AF.Sigmoid),
                             r=[kg], w=[("sg", q)])
                    self.dve(lambda e, q=q, j=j: e.tensor_tensor(out=cst[s][:, j, :], in0=bv[:, :],
                                                                 in1=sg[q][:], op=ALU.mult),
                             r=[kv, ("sg", q)], w=[("cst", s, j)])
                    yield
                self.dma(CS[:, :, 15 + c0:15 + c0 + N], cst[s][:], f"cst{s}",
                         r=[("cst", s, j) for j in range(4)], w=[("CS", i)], q="act")
                yield

            load_x(0)
            for _ in glu_gen(0):
                pass
            for it in range(T + 1):
                i = T - 1 - it
                cg = conv_gen(i + 1) if it >= 1 else None
                if it < T:
                    s = it % 2
                    c0 = i * N
                    gg_ = None
                    if it + 1 < T:
                        load_x(it + 1)
                        gg_ = glu_gen(it + 1)
                    if cg is not None:
                        next(cg)

                    def rx_src(j, s=s):
                        b = 2 + (j % 2)
                        mm_tile(s, banks[b], ("bank", b), 8 + j)
                        return banks[b], ("bank", b)
                    rg = self.emit_rnn_gen(Rn, l, 1, it, it == 0, rx_src,
                                           [((banks[4], ("bank", 4)), (banks[5], ("bank", 5)))], reverse=True)
                    for k, _ in enumerate(rg):
                        if k in (1, 3, 5, 7) and cg is not None:
                            next(cg, None)
                        if k in (0, 2, 4, 6, 8) and gg_ is not None:
                            next(gg_, None)
                    if gg_ is not None:
                        for _ in gg_:
                            pass
                    if cg is not None:
                        for _ in cg:
                            pass
                    hs = Rn["h"][it % 2]
                    self.dma(HB[:, :, c0:c0 + N], hs[:], f"hs{it % 2}", r=[("h", it % 2, j) for j in range(4)],
                             q="act")
                else:
                    for _ in cg:
                        pass
            self.P.emit_phase(f"S1_{si}_{l}")

    def phase_s2a(self, si, l):
        nc = self.nc
        S = self.seq_lens[si]
        T = S // N
        XB = self.XB[si][l].rearrange("(k p) s -> p k s", p=128)
        XF = self.XF[si][l].rearrange("(k p) s -> p k s", p=128)
        CN = self.CN[si].rearrange("(k p) s -> p k s", p=128)
        HB = self.HB[si].rearrange("(k p) s -> p k s", p=128)
        X1F = self.X1F[si].rearrange("(k p) s -> p k s", p=128)
        X1B = self.X1B[si].rearrange("(k p) s -> p k s", p=128)
        with ExitStack() as es:
            sb = lambda name, shape, dtp: es.enter_context(nc.sbuf_tensor(self.un(name), shape, dtp))
            self.epsc = sb("epsc", [128, 1], F32)
            self.onec = sb("onec", [128, 1], F32)
            self.dve(lambda e: e.memset(self.epsc[:], EPS), w=["epsc"])
            self.dve(lambda e: e.memset(self.onec[:], 1.0), w=["onec"])
            win = sb("win", [128, 8, 1024], BF16)
            wout = sb("wout", [128, 8, 1024], BF16)
            xbt = [sb(f"xbt{i}", [128, 8, N], BF16) for i in range(2)]
            R = [sb(f"R{i}", [128, 8, N], F32) for i in range(2)]
            cn = [sb(f"cn{i}", [128, 4, N], BF16) for i in range(2)]
            hbin = [sb(f"hbin{i}", [128, 4, N], F32) for i in range(1)]
            Rn = self.rnn_alloc(es)
            gg = [sb(f"gg{i}", [128, N], F32) for i in range(2)]
            rec = [sb(f"rec{i}", [128, 4, N], BF16) for i in range(2)]
            rsum = [sb(f"rsum{i}", [128, N], F32) for i in range(1)]
            x1b = [sb(f"x1b{i}", [128, 8, N], BF16) for i in range(1)]
            L = self.ln_alloc(es, "s2a")
            banks = self.psum_banks(es)
            wsrc = self.win_b[l * 128:(l + 1) * 128, :].rearrange("p (k m) -> p k m", k=8)
            wosrc = self.wout_b[l * 128:(l + 1) * 128, :].rearrange("p (k m) -> p k m", k=8)
            for h in range(2):
                self.dma(win[:, 4 * h:4 * h + 4, :], wsrc[:, 4 * h:4 * h + 4, 1024:2048], f"w{h}", w=[("win", h)])
                self.dma(wout[:, 4 * h:4 * h + 4, :], wosrc[:, 4 * h:4 * h + 4, :], f"wo{h}", w=[("wout", h)])
            wkeys = [("win", 0), ("win", 1)]
            wokeys = [("wout", 0), ("wout", 1)]
            bkm = (banks[6], ("bank", 6))
            bkq = (banks[7], ("bank", 7))

            def loads_r(i):
                s = i % 2
                c0 = i * N
                self.dma(xbt[s][:], XB[:, :, 1 + c0:1 + c0 + N], f"xbt{s}", w=[("xbt", s)])
                self.dma(hbin[0][:], HB[:, :, c0:c0 + N], "hbin0", w=[("hbin", 0, j) for j in range(4)])

            def loads_o(i):
                s = i % 2
                c0 = i * N
                self.dma(cn[s][:], CN[:, :, c0:c0 + N], f"cn{s}", w=[("cn", s, j) for j in range(4)])
                self.dma(R[s][:], XF[:, :, 1 + c0:1 + c0 + N], f"R{s}", w=[("R", s, k) for k in range(8)])

            def stage_r(i):
                s = i % 2

                def mm(bank, bkey, m):
                    for k in range(8):
                        self.pe(lambda e, k=k: e.matmul(
                            bank[:, :], lhsT=win[:, k, m * 128:(m + 1) * 128], rhs=xbt[s][:, k, :],
                            start=(k == 0), stop=(k == 7)),
                            r=[("xbt", s)] + wkeys, w=[bkey])

                def rx_src(j):
                    b = 0 + (j % 2)
                    mm(banks[b], ("bank", b), j)
                    return banks[b], ("bank", b)
                rg = self.emit_rnn_gen(Rn, l, 0, i, i == 0, rx_src,
                                       [((banks[2], ("bank", 2)), (banks[3], ("bank", 3)))], reverse=False)
                hs = Rn["h"][i % 2]

                def gate_chunk(j):
                    b = 0 + (j % 2)
                    mm(banks[b], ("bank", b), 4 + j)
                    q = self.nxt("gg", 2)
                    self.act(lambda e: e.activation(out=gg[q][:], in_=banks[b][:, :], func=AF.Gelu_apprx_tanh),
                             r=[("bank", b)], w=[("gg", q)])
                    return q

                def rec_chunk(j, q):
                    self.pool(lambda e: e.tensor_tensor(out=rsum[0][:], in0=hs[:, j, :], in1=hbin[0][:, j, :],
                                                        op=ALU.add),
                              r=[("h", i % 2, j), ("hbin", 0, j)], w=[("rsum", 0)])
                    self.dve(lambda e: e.tensor_tensor(out=rec[s][:, j, :], in0=rsum[0][:], in1=gg[q][:],
                                                       op=ALU.mult),
                             r=[("rsum", 0), ("gg", q)], w=[("rec", s, j)])
                for j in range(4):
                    next(rg)
                    yield
                qs = []
                for st in range(7):
                    next(rg)
                    if st >= 3:
                        rec_chunk(st - 3, qs[st - 3])
                    if 2 <= st <= 5:
                        qs.append(gate_chunk(st - 2))
                    if st == 6 and i + 1 < T:
                        loads_r(i + 1)
                    yield
                for _ in rg:
                    pass

            def stage_o(i):
                s = i % 2
                c0 = i * N
                pend = []
                for m in range(8):
                    b = 4 + (m % 2)
                    for k in range(8):
                        rhs = cn[s][:, k, :] if k < 4 else rec[s][:, k - 4, :]
                        rkey = ("cn", s, k) if k < 4 else ("rec", s, k - 4)
                        self.pe(lambda e, k=k, rhs=rhs, b=b, m=m: e.matmul(
                            banks[b][:, :], lhsT=wout[:, k, m * 128:(m + 1) * 128], rhs=rhs,
                            start=(k == 0), stop=(k == 7)),
                            r=[rkey] + wokeys, w=[("bank", b)])
                    self.dve(lambda e, b=b, m=m: e.scalar_tensor_tensor(
                        out=R[s][:, m, :], in0=R[s][:, m, :], scalar=ALPHA, in1=banks[b][:, :], op0=ALU.mult,
                        op1=ALU.add),
                        r=[("R", s, m), ("bank", b)], w=[("R", s, m)])
                    if m >= 1:
                        self.ln_stats_pe(L, pend.pop(0), m - 1, 8, N, 1024, bkm, bkq)
                    pend.append(self.ln_stats_act(L, R[s][:, m, :], ("R", s, m), N, 3))
                    yield
                self.ln_stats_pe(L, pend.pop(0), 7, 8, N, 1024, bkm, bkq)
                self.ln_finalize(L, N, bkm, bkq)
                yield
                for k in range(8):
                    outs_k = [(R[s][:, k, :], ("R", s, k)), (x1b[0][:, k, :], ("x1b", 0, k))]
                    self.ln_norm_chunk(L, R[s][:, k, :], ("R", s, k), k, N, f"l1g{l}", f"l1b{l}", AF.Identity, outs_k)
                    if k == 3:
                        yield
                self.dma(X1F[:, :, 1 + c0:1 + c0 + N], R[s][:], f"R{s}", r=[("R", s, k) for k in range(8)], q="act")
                self.dma(X1B[:, :, c0:c0 + N], x1b[0][:], "x1b0", r=[("x1b", 0, k) for k in range(8)], q="act")
                yield

            loads_r(0)
            prev = None
            for i in range(T):
                loads_o(i)
                for _ in stage_r(i):
                    if prev is not None:
                        next(prev, None)
                if prev is not None:
                    for _ in prev:
                        pass
                prev = stage_o(i)
            for _ in prev:
                pass
            self.P.emit_phase(f"S2a_{si}_{l}")

    def phase_s2b(self, si, l):
        nc = self.nc
        S = self.seq_lens[si]
        T = S // N
        X1F = self.X1F[si].rearrange("(k p) s -> p k s", p=128)
        X1B = self.X1B[si].rearrange("(k p) s -> p k s", p=128)
        XFo = self.XF[si][l + 1].rearrange("(k p) s -> p k s", p=128)
        XBo = self.XB[si][l + 1].rearrange("(k p) s -> p k s", p=128) if l == 0 else None
        with ExitStack() as es:
            sb = lambda name, shape, dtp: es.enter_context(nc.sbuf_tensor(self.un(name), shape, dtp))
            self.epsc = sb("epsc", [128, 1], F32)
            self.dve(lambda e: e.memset(self.epsc[:], EPS), w=["epsc"])
            x1t = [sb(f"x1t{i}", [128, 8, N], BF16) for i in range(2)]
            R = [sb(f"R{i}", [128, 8, N], F32) for i in range(2)]
            wup = [sb(f"wup{i}", [128, 8, 512], BF16) for i in range(3)]
            wdn = [sb(f"wdn{i}", [128, 22, 128], BF16) for i in range(3)]
            ub4 = [sb(f"ub4{i}", [128, 4, N + 2], F32) for i in range(2)]
            uc4 = [sb(f"uc4{i}", [128, 4, N], F32) for i in range(2)]
            gg = [sb(f"gg{i}", [128, 2, N], F32) for i in range(1)]
            gbuf = [sb(f"gbuf{i}", [128, 22, N], BF16) for i in range(2)]
            x2b = sb("x2b", [128, 8, N], BF16)
            uhist = sb("uhist", [128, NUC, 2], F32)
            L = self.ln_alloc(es, "s2b")
            pp = [es.enter_context(nc.psum_tensor(self.un(f"pp{i}"), [128, 2, 512], F32)) for i in range(2)]
            banks = self.psum_banks(es, 4)
            self.pool(lambda e: e.memset(uhist[:], 0.0), w=["uhist"])

            def conv_piece(pj, n, us, flush):
                ub = ub4[us]
                uc = uc4[us]
                for qq in range(4):
                    c = pj * 4 + qq
                    w0 = self.cv(f"fdw{l}", c * 3 + 0)
                    w1 = self.cv(f"fdw{l}", c * 3 + 1)
                    w2 = self.cv(f"fdw{l}", c * 3 + 2)
                    bc = self.cv(f"fdb{l}", c)
                    hk = ("ub4", us, qq // 2)
                    self.pool(lambda e, qq=qq, w0=w0, bc=bc: e.tensor_scalar(
                        out=uc[:, qq, :n], in0=ub[:, qq, 0:n], scalar1=w0, scalar2=bc, op0=ALU.mult, op1=ALU.add),
                        r=[hk, ("ub4h", us), "cvec"], w=[("uc4", us, qq)])
                    self.dve(lambda e, qq=qq, w1=w1: e.scalar_tensor_tensor(
                        out=uc[:, qq, :n], in0=ub[:, qq, 1:n + 1], scalar=w1, in1=uc[:, qq, :n],
                        op0=ALU.mult, op1=ALU.add),
                        r=[hk, ("ub4h", us), ("uc4", us, qq)], w=[("uc4", us, qq)])
                    self.dve(lambda e, qq=qq, w2=w2: e.scalar_tensor_tensor(
                        out=uc[:, qq, :n], in0=ub[:, qq, 2:n + 2], scalar=w2, in1=uc[:, qq, :n],
                        op0=ALU.mult, op1=ALU.add),
                        r=[hk, ("uc4", us, qq)], w=[("uc4", us, qq)])
                return uc

            def gate_piece(pj, n, us, gs):
                uc = uc4[us]
                q = 0
                self.act(lambda e: e.activation(out=gg[q][:, :, :n], in_=uc[:, 2:4, :n], func=AF.Gelu_apprx_tanh),
                         r=[("uc4", us, 2), ("uc4", us, 3)], w=[("gg", q)])
                self.dve(lambda e: e.tensor_tensor(out=gbuf[gs][:, 2 * pj:2 * pj + 2, :n], in0=gg[q][:, :, :n],
                                                   in1=uc[:, 0:2, :n], op=ALU.mult),
                          r=[("gg", q), ("uc4", us, 0), ("uc4", us, 1)], w=[("gbuf", gs, pj)])

            def stage_a(i):
                s = i % 2
                gs = i % 2
                c0 = i * N
                if i == 0:
                    self.dma(x1t[s][:], X1B[:, :, c0:c0 + N], f"x1t{s}", w=[("x1t", s)])
                for pj in range(NPC):
                    if pj == 4:
                        self.dma(R[s][:], X1F[:, :, c0:c0 + N], f"R{s}", w=[("R", s, k) for k in range(8)])
                    if pj == 6 and i + 1 < T:
                        s2 = (i + 1) % 2
                        c2 = (i + 1) * N
                        self.dma(x1t[s2][:], X1B[:, :, c2:c2 + N], f"x1t{s2}", w=[("x1t", s2)])
                    ws = self.nxt("wup", 3)
                    row = (l * NPC + pj) * 128
                    self.dma(wup[ws][:], self.wup_b[row:row + 128, :].rearrange("p (k c) -> p k c", k=8),
                             f"wup{ws}", w=[("wup", ws)])
                    us = self.nxt("ub4", 2)
                    ub = ub4[us]
                    self.pool(lambda e, ub=ub, pj=pj: e.tensor_copy(out=ub[:, :, 0:2], in_=uhist[:, 4 * pj:4 * pj + 4, :]),
                              r=["uhist"], w=[("ub4h", us)])
                    for u in range(2):
                        pb = self.nxt("pp", 2)
                        for q2 in range(2):
                            qq = 2 * u + q2
                            for k in range(8):
                                self.pe(lambda e, k=k, pb=pb, q2=q2, ws=ws, qq=qq, s=s: e.matmul(
                                    pp[pb][:, q2, :], lhsT=wup[ws][:, k, qq * 128:(qq + 1) * 128], rhs=x1t[s][:, k, :],
                                    start=(k == 0), stop=(k == 7)),
                                    r=[("x1t", s), ("wup", ws)], w=[("pp", pb)])
                        self.act(lambda e, ub=ub, u=u, pb=pb: e.activation(
                            out=ub[:, 2 * u:2 * u + 2, 2:N + 2], in_=pp[pb][:, :, :], func=AF.Copy),
                            r=[("pp", pb)], w=[("ub4", us, u)])
                    conv_piece(pj, N, us, False)
                    self.pool(lambda e, ub=ub, pj=pj: e.tensor_copy(out=uhist[:, 4 * pj:4 * pj + 4, :], in_=ub[:, :, N:N + 2]),
                              r=[("ub4", us, 0), ("ub4", us, 1)], w=["uhist"])
                    gate_piece(pj, N, us, gs)
                    yield

            def stage_bc(i, n, flush):
                rs = i % 2
                gs = i % 2
                bkm = (banks[2], ("bank", 2))
                bkq = (banks[3], ("bank", 3))
                pend = []
                for m in range(8):
                    ws = self.nxt("wdn", 3)
                    row = (l * 8 + m) * 128
                    self.dma(wdn[ws][:], self.wdn_b[row:row + 128, :].rearrange("p (k c) -> p k c", k=22),
                             f"wdn{ws}", w=[("wdn", ws)])
                    b = m % 2
                    for k in range(22):
                        self.pe(lambda e, k=k, b=b, ws=ws: e.matmul(
                            banks[b][:, :n], lhsT=wdn[ws][:, k, :], rhs=gbuf[gs][:, k, :n],
                            start=(k == 0), stop=(k == 21)),
                            r=[("gbuf", gs, k // 2), ("wdn", ws)], w=[("bank", b)])
                    self.dve(lambda e, b=b, m=m: e.scalar_tensor_tensor(
                        out=R[rs][:, m, :n], in0=R[rs][:, m, :n], scalar=ALPHA, in1=banks[b][:, :n],
                        op0=ALU.mult, op1=ALU.add),
                        r=[("R", rs, m), ("bank", b)], w=[("R", rs, m)])
                    if m >= 2:
                        self.ln_stats_pe(L, pend.pop(0), m - 2, 8, n, 1024, bkm, bkq)
                    if m >= 1:
                        pend.append(self.ln_stats_act(L, R[rs][:, m - 1, :n], ("R", rs, m - 1), n, 3))
                    yield
                self.ln_stats_pe(L, pend.pop(0), 6, 8, n, 1024, bkm, bkq)
                pend.append(self.ln_stats_act(L, R[rs][:, 7, :n], ("R", rs, 7), n, 3))
                yield
                self.ln_stats_pe(L, pend.pop(0), 7, 8, n, 1024, bkm, bkq)
                self.ln_finalize(L, n, bkm, bkq)
                for k in range(8):
                    outs_k = [(R[rs][:, k, :n], ("R", rs, k))] + ([(x2b[:, k, :n], ("x2b", k))] if l == 0 else [])
                    self.ln_norm_chunk(L, R[rs][:, k, :n], ("R", rs, k), k, n, f"l2g{l}", f"l2b{l}", AF.Identity, outs_k)
                    if k == 3:
                        yield
                c0 = S if flush else i * N
                self.dma(XFo[:, :, c0:c0 + n], R[rs][:, :, :n], f"R{rs}", r=[("R", rs, k) for k in range(8)],
                         slow=flush, q="act")
                if l == 0:
                    self.dma(XBo[:, :, c0:c0 + n], x2b[:, :, :n], "x2b", r=[("x2b", k) for k in range(8)],
                             slow=flush, q="act")
                yield

            prev = None
            for i in range(T):
                ga = stage_a(i)
                for _ in ga:
                    if prev is not None:
                        next(prev, None)
                if prev is not None:
                    for _ in prev:
                        pass
                prev = stage_bc(i, N, False)
            rs = T % 2
            gs = T % 2
            self.dma(R[rs][:, :, 0:1], X1F[:, :, S:S + 1], f"R{rs}", w=[("R", rs, k) for k in range(8)], slow=True)
            for pj in range(NPC):
                us = self.nxt("ub4", 2)
                ub = ub4[us]
                self.pool(lambda e, ub=ub, pj=pj: e.tensor_copy(out=ub[:, :, 0:2], in_=uhist[:, 4 * pj:4 * pj + 4, :]),
                          r=["uhist"], w=[("ub4h", us)])
                self.pool(lambda e, ub=ub: e.memset(ub[:, :, 2:3], 0.0), w=[("ub4", us, 0), ("ub4", us, 1)])
                conv_piece(pj, 1, us, True)
                gate_piece(pj, 1, us, gs)
                next(prev, None)
            for _ in prev:
                pass
            for _ in stage_bc(T, 1, True):
                pass
            self.P.emit_phase(f"S2b_{si}_{l}")

    def phase_out(self, si):
        nc = self.nc
        S = self.seq_lens[si]
        T = S // N
        XF = self.XF[si][2].rearrange("(k p) s -> p k s", p=128)
        y = self.y_out[si]
        with ExitStack() as es:
            sb = lambda name, shape, dtp: es.enter_context(nc.sbuf_tensor(self.un(name), shape, dtp))
            R = [sb(f"R{i}", [128, 8, N], F32) for i in range(2)]
            ot = [sb(f"ot{i}", [128, D], F32) for i in range(3)]
            pt = [es.enter_context(nc.psum_tensor(self.un(f"pt{i}"), [128, 1024], F32)) for i in range(2)]
            for i in range(T):
                rs = i % 2
                c0 = 1 + i * N
                self.dma(R[rs][:], XF[:, :, c0:c0 + N], f"R{rs}", w=[("R", rs)])
                for tb in range(4):
                    ps = self.nxt("pt", 2)
                    os_ = self.nxt("ot", 3)
                    for k in range(8):
                        self.pe(lambda e, rs=rs, ps=ps, k=k, tb=tb: e.transpose(
                            out=pt[ps][:, k * 128:(k + 1) * 128], in_=R[rs][:, k, tb * 128:(tb + 1) * 128],
                            identity=self.ident[:]),
                            r=[("R", rs), "ident"], w=[("pt", ps)])
                    for hh in range(2):
                        fn = (lambda e, os_=os_, ps=ps, hh=hh: e.activation(
                            out=ot[os_][:, hh * 512:(hh + 1) * 512], in_=pt[ps][:, hh * 512:(hh + 1) * 512],
                            func=AF.Copy))
                        if hh == 0:
                            self.act(fn, r=[("pt", ps)], w=[("ot", os_, hh)])
                        else:
                            fn2 = (lambda e, os_=os_, ps=ps, hh=hh: e.tensor_copy(
                                out=ot[os_][:, hh * 512:(hh + 1) * 512], in_=pt[ps][:, hh * 512:(hh + 1) * 512]))
                            self.dve(fn2, r=[("pt", ps)], w=[("ot", os_, hh)])
                    row0 = i * N + tb * 128
                    self.dma(y[row0:row0 + 128, :], ot[os_][:], f"ot{os_}",
                             r=[("ot", os_, 0), ("ot", os_, 1)])
            self.P.emit_phase(f"PO_{si}")


_NC_CACHE = {}


def _get_nc(seq_lens):
    key = tuple(seq_lens)
    if key not in _NC_CACHE:
        nc = bass.Bass("TRN2", target_bir_lowering=False)
        b = Builder(nc, list(seq_lens))
        b.build()
        _NC_CACHE[key] = nc
    return _NC_CACHE[key]


def run_cores(xa_list, xb_list, inp):
    SA = xa_list[0].shape[0]
    SB = xb_list[0].shape[0]
    nc = _get_nc((SA, SB))
    cvec = _pack_cvec(inp)
    wts = _pack_weights(inp)
    in_maps = []
    for a, b in zip(xa_list, xb_list):
        m = {"x0": np.ascontiguousarray(a, np.float32), "x1": np.ascontiguousarray(b, np.float32), "cvec": cvec}
        m.update(wts)
        in_maps.append(m)
    res = run_bass_kernel_spmd(nc, in_maps, core_ids=list(range(len(in_maps))))
    return [(r["y0"], r["y1"]) for r in res.results]


def kernel(**inputs):
    xp = np.asarray(inputs["x_prompt"], np.float32)
    xs = np.asarray(inputs["x_sample"], np.float32)
    xa = [xs[c] for c in range(8)]
    xb = [xp[c % 4] for c in range(8)]
    outs = run_cores(xa, xb, inputs)
    y_sample = np.stack([outs[c][0] for c in range(8)], axis=0)
    y_prompt = np.stack([outs[c][1] for c in range(4)], axis=0)
    return (y_prompt.astype(np.float32), y_sample.astype(np.float32))
```

```python
import numpy as np
from contextlib import ExitStack
import concourse.bass as bass
import concourse.mybir as mybir
from concourse.bass_utils import run_bass_kernel_spmd

F32 = mybir.dt.float32
BF16 = mybir.dt.bfloat16
AF = mybir.ActivationFunctionType
ALU = mybir.AluOpType

N = 512
D = 1024
DC = 512
DFF = 2816
NUC = 44
NPC = 11
ALPHA = float((2.0 * 2) ** 0.25)
EPS = 1e-5
LRU_C = 8.0
ENGS = ("pe", "act", "dve", "pool", "sp")

UORD = []
for _j in range(NPC):
    UORD += [2 * _j, 2 * _j + 1, 22 + 2 * _j, 22 + 2 * _j + 1]


def _cvec_layout():
    off = {}
    cur = [0]

    def add(name, w):
        off[name] = cur[0]
        cur[0] += w

    add("lig", 8)
    add("lib", 8)
    for l in range(2):
        add(f"cdb{l}", 4)
        add(f"clg{l}", 4)
        add(f"clb{l}", 4)
        add(f"cdw{l}", 4 * 31)
        add(f"rcw{l}", 2 * 4 * 4)
        add(f"rcb{l}", 8)
        add(f"rba{l}", 8)
        add(f"rbx{l}", 8)
        add(f"lam{l}", 8)
        add(f"l1g{l}", 8)
        add(f"l1b{l}", 8)
        add(f"fdw{l}", NUC * 3)
        add(f"fdb{l}", NUC)
        add(f"l2g{l}", 8)
        add(f"l2b{l}", 8)
    return off, cur[0]


CV_OFF, CV_N = _cvec_layout()


def _pc(v):
    v = np.asarray(v, np.float32)
    return np.ascontiguousarray(v.reshape(-1, 128).T)


def _pack_cvec(inp):
    cv = np.zeros((128, CV_N), np.float32)

    def put(name, arr):
        arr = np.asarray(arr, np.float32).reshape(128, -1)
        cv[:, CV_OFF[name]:CV_OFF[name] + arr.shape[1]] = arr

    put("lig", _pc(inp["ln_in_g"]))
    put("lib", _pc(inp["ln_in_b"]))
    for l in range(2):
        put(f"cdb{l}", _pc(inp["conv_dw_b"][l]))
        put(f"clg{l}", _pc(inp["conv_ln_g"][l]))
        put(f"clb{l}", _pc(inp["conv_ln_b"][l]))
        w = np.asarray(inp["conv_dw_w"][l], np.float32)
        put(f"cdw{l}", w.reshape(31, 4, 128).transpose(2, 1, 0))
        w = np.asarray(inp["rnn_conv_w"][l], np.float32)
        put(f"rcw{l}", w.reshape(2, 4, 4, 128).transpose(3, 0, 2, 1))
        w = np.asarray(inp["rnn_conv_b"][l], np.float32)
        put(f"rcb{l}", w.reshape(2, 4, 128).transpose(2, 0, 1))
        put(f"rba{l}", np.asarray(inp["rg_b_a"][l], np.float32).reshape(2, 4, 128).transpose(2, 0, 1))
        put(f"rbx{l}", np.asarray(inp["rg_b_x"][l], np.float32).reshape(2, 4, 128).transpose(2, 0, 1))
        put(f"lam{l}", np.asarray(inp["rg_lambda"][l], np.float32).reshape(2, 4, 128).transpose(2, 0, 1))
        put(f"l1g{l}", _pc(inp["ln1_g"][l]))
        put(f"l1b{l}", _pc(inp["ln1_b"][l]))
        w = np.asarray(inp["ffn_dw_w"][l], np.float32).reshape(3, NUC, 128)[:, UORD, :]
        put(f"fdw{l}", w.transpose(2, 1, 0))
        w = np.asarray(inp["ffn_dw_b"][l], np.float32).reshape(NUC, 128)[UORD, :]
        put(f"fdb{l}", w.T)
        put(f"l2g{l}", _pc(inp["ln2_g"][l]))
        put(f"l2b{l}", _pc(inp["ln2_b"][l]))
    return cv


def _pack_weights(inp):
    out = {}
    w_in = np.asarray(inp["w_in"], np.float32)
    out["win_h"] = np.ascontiguousarray(
        w_in.reshape(2, 8, 128, 2048).transpose(0, 2, 1, 3)).reshape(2 * 128, 8 * 2048)
    w_out = np.asarray(inp["w_out"], np.float32)
    out["wout_h"] = np.ascontiguousarray(
        w_out.reshape(2, 8, 128, 1024).transpose(0, 2, 1, 3)).reshape(2 * 128, 8 * 1024)
    w_up = np.asarray(inp["w_up"], np.float32)
    wu = w_up.reshape(2, 8, 128, NUC, 128)[:, :, :, UORD, :]
    wu = wu.reshape(2, 8, 128, NPC, 512).transpose(0, 3, 2, 1, 4)
    out["wup_h"] = np.ascontiguousarray(wu).reshape(2 * NPC * 128, 8 * 512)
    w_dn = np.asarray(inp["w_down"], np.float32)
    wd = w_dn.reshape(2, 22, 128, 8, 128).transpose(0, 3, 2, 1, 4)
    out["wdn_h"] = np.ascontiguousarray(wd).reshape(2 * 8 * 128, 22 * 128)
    g = np.zeros((128, 32, 128), np.float32)
    for l in range(2):
        for d in range(2):
            for ti, nm in enumerate(("rg_w_a", "rg_w_x")):
                w = np.asarray(inp[nm][l][d], np.float32)
                for j in range(4):
                    idx = ((l * 2 + d) * 2 + ti) * 4 + j
                    g[0:64, idx, 0:64] = w[2 * j]
                    g[64:128, idx, 64:128] = w[2 * j + 1]
    out["gate_h"] = g.reshape(128, 32 * 128)
    out["ident"] = np.eye(128, dtype=np.float32)
    return out


class Op:
    __slots__ = ("eng", "fn", "deps", "signal", "ticket", "sem", "idx", "dma")

    def __init__(self, eng, fn):
        self.eng = eng
        self.fn = fn
        self.deps = []
        self.signal = False
        self.ticket = 0
        self.sem = None
        self.idx = 0
        self.dma = False


class Prog:
    def __init__(self, nc):
        self.nc = nc
        self.eng_sem = {}
        self.eng_cnt = {e: 0 for e in ENGS}
        self.eng_nops = {e: 0 for e in ENGS}
        self.dma_sems = {}
        self.dma_cnt = {}
        self.reset_phase()

    def reset_phase(self):
        self.ops = {e: [] for e in ENGS}
        self.last_w = {}
        self.readers = {}
        self.dma_last = {}

    def sem_for_engine(self, e):
        if e not in self.eng_sem:
            self.eng_sem[e] = self.nc.alloc_semaphore(name=f"sem_{e}")
        return self.eng_sem[e]

    def sem_for_dma(self, key):
        if key not in self.dma_sems:
            self.dma_sems[key] = self.nc.alloc_semaphore(name=f"dsem_{key}")
            self.dma_cnt[key] = 0
        return self.dma_sems[key]

    def _need(self, d, op):
        if d.dma or op.dma:
            return True
        if d.eng != op.eng:
            return True
        if op.eng == "pe":
            return False
        return (op.idx - d.idx) <= 3

    def add(self, eng, fn, reads=(), writes=(), dma_key=None):
        op = Op(eng, fn)
        op.idx = self.eng_nops[eng]
        self.eng_nops[eng] += 1
        cand = []
        for k in reads:
            w = self.last_w.get(k)
            if w is not None:
                cand.append(w)
        for k in writes:
            w = self.last_w.get(k)
            if w is not None:
                cand.append(w)
            cand.extend(self.readers.get(k, ()))
        if dma_key is not None:
            op.dma = True
            op.sem = self.sem_for_dma(dma_key)
            prev = self.dma_last.get(dma_key)
            if prev is not None:
                cand.append(prev)
            self.dma_cnt[dma_key] += 1
            op.ticket = 16 * self.dma_cnt[dma_key]
            op.signal = True
            self.dma_last[dma_key] = op
        else:
            op.sem = self.sem_for_engine(eng)
        best = {}
        for d in cand:
            if d is op or not self._need(d, op):
                continue
            key = ("d", id(d.sem)) if d.dma else ("e", d.eng)
            b = best.get(key)
            if b is None or (d.ticket > b.ticket if d.dma else d.idx > b.idx):
                best[key] = d
        op.deps = list(best.values())
        for d in op.deps:
            d.signal = True
        for k in reads:
            lst = self.readers.setdefault(k, [])
            if not op.dma:
                lst[:] = [r for r in lst if r.dma or r.eng != eng]
            lst.append(op)
        for k in writes:
            self.last_w[k] = op
            self.readers[k] = []
        self.ops[eng].append(op)
        return op

    def emit_phase(self, name="phase"):
        nc = self.nc
        for e in ENGS:
            for op in self.ops[e]:
                if not op.dma and op.signal:
                    self.eng_cnt[e] += 1
                    op.ticket = self.eng_cnt[e]
        ops = self.ops

        def run(ename, eng):
            waited = {}
            for op in ops[ename]:
                for d in op.deps:
                    sid = id(d.sem)
                    if waited.get(sid, 0) < d.ticket:
                        eng.wait_ge(d.sem, d.ticket)
                        waited[sid] = d.ticket
                ins = op.fn(eng)
                if op.signal:
                    ins.then_inc(op.sem, 16 if op.dma else 1)
            if ename == "sp":
                for k, sem in self.dma_sems.items():
                    v = 16 * self.dma_cnt[k]
                    if v > 0 and waited.get(id(sem), 0) < v:
                        eng.wait_ge(sem, v)

        with nc.named_scope(name), nc.Block() as blk:
            blk.tensor(lambda e: run("pe", e))
            blk.scalar(lambda e: run("act", e))
            blk.vector(lambda e: run("dve", e))
            blk.gpsimd(lambda e: run("pool", e))
            blk.sync(lambda e: run("sp", e))
        self.reset_phase()

    def emit_final_fence(self):
        nc = self.nc
        sems = [(self.dma_sems[k], 16 * self.dma_cnt[k]) for k in self.dma_sems]

        with nc.Block() as blk:
            def f(e):
                for s, v in sems:
                    if v > 0:
                        e.wait_ge(s, v)
            blk.sync(f)


class Builder:
    def __init__(self, nc, seq_lens):
        self.nc = nc
        self.P = Prog(nc)
        self.seq_lens = seq_lens
        self.rot = {}

    def un(self, name):
        self.uid = getattr(self, "uid", 0) + 1
        return f"{name}_u{self.uid}"

    def nxt(self, name, n):
        v = self.rot.get(name, 0)
        self.rot[name] = v + 1
        return v % n

    def pe(self, fn, r=(), w=()):
        return self.P.add("pe", fn, r, w)

    def act(self, fn, r=(), w=()):
        return self.P.add("act", fn, r, w)

    def dve(self, fn, r=(), w=()):
        return self.P.add("dve", fn, r, w)

    def pool(self, fn, r=(), w=()):
        return self.P.add("pool", fn, r, w)

    def dma(self, out, in_, key, r=(), w=(), slow=False, q="sp"):
        if slow:
            fn = lambda e: e.dma_start(out=out, in_=in_, allow_slow_non_contiguous=True)
        else:
            fn = lambda e: e.dma_start(out=out, in_=in_)
        return self.P.add(q, fn, r, w, dma_key=key)

    def cv(self, name, col, width=1):
        o = CV_OFF[name] + col
        return self.cvec[:, o:o + width]

    def declare(self):
        nc = self.nc
        dt = nc.dram_tensor
        self.x_in = []
        self.y_out = []
        for si, S in enumerate(self.seq_lens):
            self.x_in.append(dt(f"x{si}", [S, D], F32, kind="ExternalInput").ap())
            self.y_out.append(dt(f"y{si}", [S, D], F32, kind="ExternalOutput").ap())
        self.cvec_h = dt("cvec", [128, CV_N], F32, kind="ExternalInput").ap()
        self.win_h = dt("win_h", [256, 8 * 2048], F32, kind="ExternalInput").ap()
        self.wout_h = dt("wout_h", [256, 8 * 1024], F32, kind="ExternalInput").ap()
        self.wup_h = dt("wup_h", [2 * NPC * 128, 4096], F32, kind="ExternalInput").ap()
        self.wdn_h = dt("wdn_h", [2 * 8 * 128, 2816], F32, kind="ExternalInput").ap()
        self.gate_h = dt("gate_h", [128, 4096], F32, kind="ExternalInput").ap()
        self.ident_h = dt("ident", [128, 128], F32, kind="ExternalInput").ap()
        self.win_b = dt("win_b", [256, 8 * 2048], BF16, kind="Internal").ap()
        self.wout_b = dt("wout_b", [256, 8 * 1024], BF16, kind="Internal").ap()
        self.wup_b = dt("wup_b", [2 * NPC * 128, 4096], BF16, kind="Internal").ap()
        self.wdn_b = dt("wdn_b", [2 * 8 * 128, 2816], BF16, kind="Internal").ap()
        self.XF = []
        self.XB = []
        self.X1F = []
        self.X1B = []
        self.CS = []
        self.CN = []
        self.HB = []
        for si, S in enumerate(self.seq_lens):
            self.XF.append([dt(f"xf{si}_{j}", [D, S + 1], F32, kind="Internal").ap() for j in range(3)])
            self.XB.append([dt(f"xb{si}_{j}", [D, S + 1], BF16, kind="Internal").ap() for j in range(2)])
            self.X1F.append(dt(f"x1f{si}", [D, S + 1], F32, kind="Internal").ap())
            self.X1B.append(dt(f"x1b{si}", [D, S], BF16, kind="Internal").ap())
            self.CS.append(dt(f"cs{si}", [DC, S + 30], BF16, kind="Internal").ap())
            self.CN.append(dt(f"cn{si}", [DC, S], BF16, kind="Internal").ap())
            self.HB.append(dt(f"hb{si}", [DC, S], F32, kind="Internal").ap())

    def build(self):
        nc = self.nc
        self.declare()
        with ExitStack() as gs:
            sb = lambda name, shape, dtp: gs.enter_context(nc.sbuf_tensor(name, shape, dtp))
            self.cvec = sb("cvec_sb", [128, CV_N], F32)
            self.lamc = sb("lamc", [128, 16], F32)
            self.ident = sb("ident_sb", [128, 128], F32)
            self.ones512 = sb("ones512", [128, 128], BF16)
            self.ones1024 = sb("ones1024", [128, 128], BF16)
            self.gates = sb("gates_sb", [128, 32, 128], BF16)
            self.zeros = sb("zeros_sb", [128, 64], F32)
            self.hbias = sb("hbias_sb", [128, 32], F32)
            self.lamh = sb("lamh_sb", [128, 16], F32)
            self.qc = sb("qc_sb", [128, 1], F32)
            self.zerosb = sb("zerosb_sb", [128, 64], BF16)
            for si in range(len(self.seq_lens)):
                self.phase_p0(si, with_weights=(si == 0))
                for l in range(2):
                    self.phase_s1(si, l)
                    self.phase_s2a(si, l)
                    self.phase_s2b(si, l)
                self.phase_out(si)
            self.P.emit_final_fence()

    def psum_banks(self, es, n=8):
        return [es.enter_context(self.nc.psum_tensor(self.un(f"bank{i}"), [128, 512], F32)) for i in range(n)]

    def weights_gen(self, es):
        nc = self.nc
        if True:
            sb = lambda name, shape, dtp: es.enter_context(nc.sbuf_tensor(self.un(name), shape, dtp))
            fin = [sb(f"fin{i}", [128, 2048], F32) for i in range(4)]
            fout = [sb(f"fout{i}", [128, 2048], BF16) for i in range(4)]
            tmp = sb("wtmp", [128, 16], F32)
            self.dma(self.cvec[:], self.cvec_h, "c0", w=["cvec"])
            self.dma(self.ident[:], self.ident_h, "c1", w=["ident"])
            self.dve(lambda e: e.memset(self.ones512[:], 1.0 / 512.0), w=["ones"])
            self.dve(lambda e: e.memset(self.ones1024[:], 1.0 / 1024.0), w=["ones"])
            self.dve(lambda e: e.memset(self.zeros[:], 0.0), w=["zeros"])
            self.dve(lambda e: e.memset(self.zerosb[:], 0.0), w=["zeros"])
            for l in range(2):
                lam = self.cv(f"lam{l}", 0, 8)
                dst = self.lamc[:, 8 * l:8 * l + 8]
                t = tmp[:, 8 * l:8 * l + 8]
                self.act(lambda e, t=t, lam=lam: e.activation(out=t, in_=lam, func=AF.Exp, scale=-1.0),
                         r=["cvec"], w=[("wtmp", l)])
                self.act(lambda e, t=t: e.activation(out=t, in_=t, func=AF.Ln, bias=1.0),
                         r=[("wtmp", l)], w=[("wtmp", l)])
                self.dve(lambda e, t=t, dst=dst: e.tensor_scalar(out=dst, in0=t, scalar1=-LRU_C, scalar2=None,
                                                                 op0=ALU.mult),
                         r=[("wtmp", l)], w=["lamc"])
            self.dve(lambda e: e.tensor_scalar(out=self.lamh[:], in0=self.lamc[:], scalar1=0.5, scalar2=None,
                                               op0=ALU.mult), r=["lamc"], w=["lamh"])
            self.dve(lambda e: e.memset(self.qc[:], 0.25), w=["qc"])
            for l in range(2):
                for ti, nm in enumerate(("rba", "rbx")):
                    dst = self.hbias[:, (l * 2 + ti) * 8:(l * 2 + ti) * 8 + 8]
                    srcc = self.cv(f"{nm}{l}", 0, 8)
                    self.dve(lambda e, dst=dst, srcc=srcc: e.tensor_scalar(out=dst, in0=srcc, scalar1=0.5, scalar2=None,
                                                                           op0=ALU.mult), r=["cvec"], w=["hbias"])
            yield
            jobs = []
            for src, dst in ((self.win_h, self.win_b), (self.wout_h, self.wout_b),
                             (self.wup_h, self.wup_b), (self.wdn_h, self.wdn_b)):
                R, C = src.shape
                for r0 in range(0, R, 128):
                    for c0 in range(0, C, 2048):
                        c1 = min(C, c0 + 2048)
                        jobs.append((src[r0:r0 + 128, c0:c1], dst[r0:r0 + 128, c0:c1], c1 - c0))
            gflat = self.gates[:].rearrange("p a b -> p (a b)")
            for c0 in range(0, 4096, 2048):
                jobs.append((self.gate_h[:, c0:c0 + 2048], gflat[:, c0:c0 + 2048], -2048))
            LOOK = 3
            for i in range(min(LOOK, len(jobs))):
                self.dma(fin[i % 4][:, :abs(jobs[i][2])], jobs[i][0], f"fin{i % 4}", w=[("fin", i % 4)], q="pool")
            for i, (src, dst, w) in enumerate(jobs):
                s = i % 4
                to_sbuf = w < 0
                w = abs(w)
                o = dst if to_sbuf else fout[s][:, :w]
                wk = ["gates"] if to_sbuf else [("fout", s)]
                self.dve(lambda e, o=o, s=s, w=w: e.tensor_copy(out=o, in_=fin[s][:, :w]),
                         r=[("fin", s)], w=wk)
                if not to_sbuf:
                    self.dma(dst, fout[s][:, :w], f"fout{s}", r=[("fout", s)], q="pool")
                if i + LOOK < len(jobs):
                    i2 = i + LOOK
                    self.dma(fin[i2 % 4][:, :abs(jobs[i2][2])], jobs[i2][0], f"fin{i2 % 4}", w=[("fin", i2 % 4)], q="pool")
                yield
            for si, S in enumerate(self.seq_lens):
                cs = self.CS[si].rearrange("(k p) s -> p k s", p=128)
                self.dma(cs[:, :, 0:15], self.zerosb[:, 0:60].rearrange("p (k s) -> p k s", k=4), "z0",
                         r=["zeros"], slow=True)
                self.dma(cs[:, :, S + 15:S + 30], self.zerosb[:, 0:60].rearrange("p (k s) -> p k s", k=4), "z1",
                         r=["zeros"], slow=True)
                x1 = self.X1F[si].rearrange("(k p) s -> p k s", p=128)
                self.dma(x1[:, :, 0:1], self.zeros[:, 0:8].rearrange("p (k s) -> p k s", k=8), "z2",
                         r=["zeros"], slow=True)
            yield

    def ln_alloc(self, es, tag):
        nc = self.nc
        sb = lambda name, shape, dtp: es.enter_context(nc.sbuf_tensor(self.un(f"{tag}_{name}"), shape, dtp))
        L = {}
        L["ybf"] = [sb(f"ybf{i}", [128, N], BF16) for i in range(3)]
        L["ysq"] = [sb(f"ysq{i}", [128, N], BF16) for i in range(3)]
        L["mean"] = sb("mean", [128, N], F32)
        L["m2"] = sb("m2", [128, N], F32)
        L["var"] = L["m2"]
        L["rstd"] = sb("rstd", [128, N], F32)
        L["t"] = [sb(f"t{i}", [128, N], F32) for i in range(2)]
        return L

    def emit_ln(self, *args, **kw):
        for _ in self.emit_ln_gen(*args, **kw):
            pass

    def ln_stats_act(self, L, y, key, n, nslots=2, pool_copy=False):
        s = self.nxt("lnst", nslots)
        ybf = L["ybf"][s]
        ysq = L["ysq"][s]
        if pool_copy:
            self.pool(lambda e: e.tensor_copy(out=ybf[:, :n], in_=y), r=[key], w=[("ybf", s)])
        else:
            self.act(lambda e: e.activation(out=ybf[:, :n], in_=y, func=AF.Copy), r=[key], w=[("ybf", s)])
        self.act(lambda e: e.activation(out=ysq[:, :n], in_=y, func=AF.Square), r=[key], w=[("ysq", s)])
        return s

    def ln_stats_pe(self, L, s, k, nch, n, C, bank_m, bank_q):
        ones = self.ones512 if C == 512 else self.ones1024
        bm, km = bank_m
        bq, kq = bank_q
        ybf = L["ybf"][s]
        ysq = L["ysq"][s]
        self.pe(lambda e: e.matmul(bm[:, :n], lhsT=ones[:], rhs=ybf[:, :n], start=(k == 0), stop=(k == nch - 1)),
                r=[("ybf", s), "ones"], w=[km])
        self.pe(lambda e: e.matmul(bq[:, :n], lhsT=ones[:], rhs=ysq[:, :n], start=(k == 0), stop=(k == nch - 1)),
                r=[("ysq", s), "ones"], w=[kq])

    def ln_stats_chunk(self, L, y, key, k, nch, n, C, bank_m, bank_q):
        s = self.ln_stats_act(L, y, key, n)
        self.ln_stats_pe(L, s, k, nch, n, C, bank_m, bank_q)

    def ln_finalize(self, L, n, bank_m, bank_q):
        bm, km = bank_m
        bq, kq = bank_q
        mean, m2, var, rstd = L["mean"], L["m2"], L["var"], L["rstd"]
        self.act(lambda e: e.activation(out=mean[:, :n], in_=bm[:, :n], func=AF.Copy), r=[km], w=["ln_mean"])
        self.act(lambda e: e.activation(out=m2[:, :n], in_=bm[:, :n], func=AF.Square), r=[km], w=["ln_m2"])
        self.dve(lambda e: e.tensor_tensor(out=var[:, :n], in0=bq[:, :n], in1=m2[:, :n], op=ALU.subtract),
                 r=[kq, "ln_m2"], w=["ln_m2"])
        self.act(lambda e: e.activation(out=var[:, :n], in_=var[:, :n], func=AF.Sqrt, bias=self.epsc[:, 0:1]),
                 r=["ln_m2"], w=["ln_m2"])
        self.dve(lambda e: e.reciprocal(out=rstd[:, :n], in_=var[:, :n]), r=["ln_m2"], w=["ln_rstd"])

    def ln_norm_chunk(self, L, y, key, k, n, gname, bname, func, outs_k, pool_only=False, pool_dup=False):
        mean, rstd = L["mean"], L["rstd"]
        s = self.nxt("lnt", 2)
        t = L["t"][s]
        first = self.pool if ((k % 2 == 0 and not getattr(self, "ln_no_pool", False)) or pool_only) else self.dve
        second = self.pool if pool_only else self.dve
        first(lambda e: e.tensor_tensor(out=t[:, :n], in0=y, in1=mean[:, :n], op=ALU.subtract),
              r=[key, "ln_mean"], w=[("lnt", s)])
        second(lambda e: e.tensor_tensor(out=t[:, :n], in0=t[:, :n], in1=rstd[:, :n], op=ALU.mult),
               r=[("lnt", s), "ln_rstd"], w=[("lnt", s)])
        g = self.cv(gname, k)
        b = self.cv(bname, k)
        if pool_dup and len(outs_k) == 2:
            (o0, k0), (o1, k1) = outs_k
            self.act(lambda e: e.activation(out=o0, in_=t[:, :n], func=func, bias=b, scale=g),
                     r=[("lnt", s), "cvec"], w=[k0])
            self.pool(lambda e: e.tensor_copy(out=o1, in_=o0), r=[k0], w=[k1])
            return
        for (o, okey) in outs_k:
            self.act(lambda e, o=o: e.activation(out=o, in_=t[:, :n], func=func, bias=b, scale=g),
                     r=[("lnt", s), "cvec"], w=[okey])

    def emit_ln_gen(self, L, ys, n, C, gname, bname, func, outs, bank_m, bank_q, pool_help=False):
        nch = len(ys)
        for k, (y, key) in enumerate(ys):
            s_ = self.ln_stats_act(L, y, key, n, 2, pool_copy=pool_help)
            self.ln_stats_pe(L, s_, k, nch, n, C, bank_m, bank_q)
        yield
        self.ln_finalize(L, n, bank_m, bank_q)
        yield
        for k, (y, key) in enumerate(ys):
            self.ln_norm_chunk(L, y, key, k, n, gname, bname, func, outs[k], pool_dup=pool_help)
            if k % 4 == 3 and k != nch - 1:
                yield

    def phase_p0(self, si, with_weights=False):
        nc = self.nc
        S = self.seq_lens[si]
        T = S // N
        x = self.x_in[si]
        XF = self.XF[si][0].rearrange("(k p) s -> p k s", p=128)
        XB = self.XB[si][0].rearrange("(k p) s -> p k s", p=128)
        with ExitStack() as es:
            sb = lambda name, shape, dtp: es.enter_context(nc.sbuf_tensor(self.un(name), shape, dtp))
            extra = None
            self.ln_no_pool = with_weights
            if with_weights:
                extra = self.weights_gen(es)
                next(extra)
            self.epsc = sb("epsc", [128, 1], F32)
            self.dve(lambda e: e.memset(self.epsc[:], EPS), w=["epsc"])
            xin = [sb(f"xin{i}", [128, D], F32) for i in range(3)]
            R = [sb(f"R{i}", [128, 8, N], F32) for i in range(2)]
            RB = [sb(f"RB{i}", [128, 8, N], BF16) for i in range(2)]
            L = self.ln_alloc(es, "p0")
            pt = [es.enter_context(nc.psum_tensor(self.un(f"pt{i}"), [128, 1024], F32)) for i in range(2)]
            bm = es.enter_context(nc.psum_tensor(self.un("bm"), [128, 512], F32))
            bq = es.enter_context(nc.psum_tensor(self.un("bq"), [128, 512], F32))
            def stage_t(i):
                rs = i % 2
                for tb in range(4):
                    xs = self.nxt("xin", 3)
                    ps = self.nxt("pt", 2)
                    row0 = i * N + tb * 128
                    self.dma(xin[xs][:], x[row0:row0 + 128, :], f"xin{xs}", w=[("xin", xs)])
                    for k in range(8):
                        self.pe(lambda e, xs=xs, ps=ps, k=k: e.transpose(
                            out=pt[ps][:, k * 128:(k + 1) * 128], in_=xin[xs][:, k * 128:(k + 1) * 128],
                            identity=self.ident[:]),
                            r=[("xin", xs), "ident"], w=[("pt", ps)])
                    src_v = pt[ps][:].rearrange("p (k c) -> p k c", k=8)
                    dst_v = R[rs][:, :, tb * 128:(tb + 1) * 128]
                    if tb % 2 == 0:
                        self.act(lambda e, src_v=src_v, dst_v=dst_v: e.activation(out=dst_v, in_=src_v, func=AF.Copy),
                                 r=[("pt", ps)], w=[("R", rs, k) for k in range(8)])
                    else:
                        self.dve(lambda e, src_v=src_v, dst_v=dst_v: e.tensor_copy(out=dst_v, in_=src_v),
                                 r=[("pt", ps)], w=[("R", rs, k) for k in range(8)])
                    yield

            def stage_l(i):
                rs = i % 2
                ys = [(R[rs][:, k, :], ("R", rs, k)) for k in range(8)]
                outs = [[(R[rs][:, k, :], ("R", rs, k)), (RB[rs][:, k, :], ("RB", rs, k))] for k in range(8)]
                for _ in self.emit_ln_gen(L, ys, N, 1024, "lig", "lib", AF.Identity, outs, (bm, "bm"), (bq, "bq"),
                                          pool_help=False):
                    yield
                c0 = 1 + i * N
                self.dma(XF[:, :, c0:c0 + N], R[rs][:], f"R{rs}", r=[("R", rs, k) for k in range(8)], q="act")
                self.dma(XB[:, :, c0:c0 + N], RB[rs][:], f"RB{rs}", r=[("RB", rs, k) for k in range(8)], q="act")
                yield

            prev = None
            for i in range(T):
                for _ in stage_t(i):
                    if prev is not None:
                        next(prev, None)
                    if extra is not None:
                        next(extra, None)
                        next(extra, None)
                if prev is not None:
                    for _ in prev:
                        pass
                prev = stage_l(i)
            for _ in prev:
                pass
            if extra is not None:
                for _ in extra:
                    pass
            self.ln_no_pool = False
            self.P.emit_phase(f"P0_{si}")

    def rnn_alloc(self, es):
        nc = self.nc
        sb = lambda name, shape, dtp: es.enter_context(nc.sbuf_tensor(self.un(name), shape, dtp))
        Rn = {}
        Rn["rbuf"] = [sb(f"rbuf{i}", [128, 4, N + 3], F32) for i in range(2)]
        Rn["xcv"] = [sb(f"xcv{i}", [128, N], F32) for i in range(3)]
        Rn["xcb"] = [sb(f"xcb{i}", [128, N], BF16) for i in range(2)]
        Rn["rr"] = [sb(f"rr{i}", [128, N], F32) for i in range(2)]
        Rn["ig"] = [sb(f"ig{i}", [128, N], F32) for i in range(2)]
        Rn["aa"] = [sb(f"aa{i}", [128, N], F32) for i in range(2)]
        Rn["sq"] = [sb(f"sq{i}", [128, N], F32) for i in range(2)]
        Rn["bb"] = [sb(f"bb{i}", [128, N], F32) for i in range(2)]
        Rn["h"] = [sb(f"hs{i}", [128, 4, N], F32) for i in range(2)]
        return Rn

    def emit_rnn(self, *a, **k):
        for _ in self.emit_rnn_gen(*a, **k):
            pass

    def emit_rnn_gen(self, Rn, l, d, i, first, rx_src, g_banks, reverse):
        s = i % 2
        o = 1 - s
        rbuf = Rn["rbuf"][s]
        rprev = Rn["rbuf"][o]
        off = 0 if reverse else 3
        for j in range(4):
            bank, bkey = rx_src(j)
            self.dve(lambda e, j=j, bank=bank: e.tensor_copy(out=rbuf[:, j, off:off + N], in_=bank[:, :]),
                     r=[bkey], w=[("rbuf", s, j)])
            if reverse:
                dst = rbuf[:, j, N:N + 3]
                src = rprev[:, j, 0:3]
            else:
                dst = rbuf[:, j, 0:3]
                src = rprev[:, j, N:N + 3]
            if first:
                self.pool(lambda e, dst=dst: e.memset(dst, 0.0), w=[("rbuf", s, j)])
            else:
                self.pool(lambda e, dst=dst, src=src: e.tensor_copy(out=dst, in_=src),
                          r=[("rbuf", o, j)], w=[("rbuf", s, j)])
            yield
        hs = Rn["h"][s]
        hprev = Rn["h"][o]

        def part_x(j):
            q = j % 2
            q3 = j % 3
            xcv = Rn["xcv"][q3]
            xcb = Rn["xcb"][q]
            wof = (d * 4 + j) * 4
            bcol = self.cv(f"rcb{l}", d * 4 + j)
            w0 = self.cv(f"rcw{l}", wof + 0)
            self.dve(lambda e: e.tensor_scalar(
                out=xcv[:], in0=rbuf[:, j, 0:N], scalar1=w0, scalar2=bcol, op0=ALU.mult, op1=ALU.add),
                r=[("rbuf", s, j), "cvec"], w=[("xcv", q3)])
            for k in range(1, 4):
                wk = self.cv(f"rcw{l}", wof + k)
                self.dve(lambda e, k=k, wk=wk: e.scalar_tensor_tensor(
                    out=xcv[:], in0=rbuf[:, j, k:k + N], scalar=wk, in1=xcv[:], op0=ALU.mult, op1=ALU.add),
                    r=[("rbuf", s, j), ("xcv", q3)], w=[("xcv", q3)])
            self.pool(lambda e: e.tensor_copy(out=xcb[:], in_=xcv[:]), r=[("xcv", q3)], w=[("xcb", q)])

        def part_x1b(j):
            q = j % 2
            xcb, rr, ig, aa = (Rn[nm][q] for nm in ("xcb", "rr", "ig", "aa"))
            (ga, gak), (gx, gxk) = g_banks[self.nxt("gb", len(g_banks))]
            ia = ((l * 2 + d) * 2 + 0) * 4 + j
            ix = ((l * 2 + d) * 2 + 1) * 4 + j
            self.pe(lambda e: e.matmul(ga[:, :], lhsT=self.gates[:, ia, :], rhs=xcb[:], start=True, stop=True),
                    r=[("xcb", q), "gates"], w=[gak])
            self.pe(lambda e: e.matmul(gx[:, :], lhsT=self.gates[:, ix, :], rhs=xcb[:], start=True, stop=True),
                    r=[("xcb", q), "gates"], w=[gxk])
            cb = d * 4 + j
            ba = self.hbias[:, (l * 2 + 0) * 8 + cb:(l * 2 + 0) * 8 + cb + 1]
            bx = self.hbias[:, (l * 2 + 1) * 8 + cb:(l * 2 + 1) * 8 + cb + 1]
            lh = self.lamh[:, 8 * l + cb:8 * l + cb + 1]
            self.act(lambda e: e.activation(out=rr[:], in_=ga[:, :], func=AF.Tanh, bias=ba, scale=0.5),
                     r=[gak, "hbias"], w=[("rr", q)])
            self.act(lambda e: e.activation(out=ig[:], in_=gx[:, :], func=AF.Tanh, bias=bx, scale=0.5),
                     r=[gxk, "hbias"], w=[("ig", q)])
            self.act(lambda e: e.activation(out=aa[:], in_=rr[:], func=AF.Exp, bias=lh, scale=lh),
                     r=[("rr", q), "lamh"], w=[("aa", q)])

        def part_x2(j):
            q = j % 2
            aa, sq = (Rn[nm][q] for nm in ("aa", "sq"))
            self.act(lambda e: e.activation(out=sq[:], in_=aa[:], func=AF.Square),
                     r=[("aa", q)], w=[("sq", q)])
            self.act(lambda e: e.activation(out=sq[:], in_=sq[:], func=AF.Sqrt, bias=self.qc[:, 0:1], scale=-0.25),
                     r=[("sq", q)], w=[("sq", q)])


        def part_y(j):
            q = j % 2
            aa, sq, bb, ig = (Rn[nm][q] for nm in ("aa", "sq", "bb", "ig"))
            xcv = Rn["xcv"][j % 3]
            self.dve(lambda e: e.scalar_tensor_tensor(out=bb[:], in0=ig[:], scalar=1.0, in1=xcv[:], op0=ALU.add,
                                                      op1=ALU.mult),
                     r=[("ig", q), ("xcv", j % 3)], w=[("bb", q)])
            self.dve(lambda e: e.tensor_tensor(out=bb[:], in0=bb[:], in1=sq[:], op=ALU.mult),
                     r=[("bb", q), ("sq", q)], w=[("bb", q)])
            if reverse:
                init = 0.0 if first else hprev[:, j, 0:1]
                self.dve(lambda e: e.tensor_tensor_scan(
                    out=hs[:, j, ::-1], data0=aa[:, ::-1], data1=bb[:, ::-1], initial=init,
                    op0=ALU.mult, op1=ALU.add),
                    r=[("aa", q), ("bb", q), ("h", o, j)], w=[("h", s, j)])
            else:
                init = 0.0 if first else hprev[:, j, N - 1:N]
                self.dve(lambda e: e.tensor_tensor_scan(
                    out=hs[:, j, :], data0=aa[:], data1=bb[:], initial=init, op0=ALU.mult, op1=ALU.add),
                    r=[("aa", q), ("bb", q), ("h", o, j)], w=[("h", s, j)])

        for st in range(7):
            if 3 <= st:
                part_y(st - 3)
            if 2 <= st <= 5:
                part_x2(st - 2)
            if 1 <= st <= 4:
                part_x1b(st - 1)
            if st < 4:
                part_x(st)
            yield

    def phase_s1(self, si, l):
        nc = self.nc
        S = self.seq_lens[si]
        T = S // N
        XB = self.XB[si][l].rearrange("(k p) s -> p k s", p=128)
        CS = self.CS[si].rearrange("(k p) s -> p k s", p=128)
        CN = self.CN[si].rearrange("(k p) s -> p k s", p=128)
        HB = self.HB[si].rearrange("(k p) s -> p k s", p=128)
        with ExitStack() as es:
            sb = lambda name, shape, dtp: es.enter_context(nc.sbuf_tensor(self.un(name), shape, dtp))
            self.epsc = sb("epsc", [128, 1], F32)
            self.onec = sb("onec", [128, 1], F32)
            self.dve(lambda e: e.memset(self.epsc[:], EPS), w=["epsc"])
            self.dve(lambda e: e.memset(self.onec[:], 1.0), w=["onec"])
            win = sb("win", [128, 8, 1536], BF16)
            dg = sb("dg", [128, 4, 31, 128], BF16)
            xbt = [sb(f"xbt{i}", [128, 8, N], BF16) for i in range(2)]
            cst = [sb(f"cst{i}", [128, 4, N], BF16) for i in range(2)]
            sg = [sb(f"sg{i}", [128, N], F32) for i in range(2)]
            cin = [sb(f"cin{i}", [128, 4, N + 30], BF16) for i in range(2)]
            cc = sb("cc", [128, 4, N], F32)
            cn = [sb(f"cn{i}", [128, 4, N], BF16) for i in range(2)]
            Rn = self.rnn_alloc(es)
            L = self.ln_alloc(es, "s1")
            banks = self.psum_banks(es)
            wsrc = self.win_b[l * 128:(l + 1) * 128, :].rearrange("p (k m) -> p k m", k=8)
            for h in range(2):
                self.dma(win[:, 4 * h:4 * h + 4, :], wsrc[:, 4 * h:4 * h + 4, 0:1536], f"w{h}", w=[("win", h)])
            wkeys = [("win", 0), ("win", 1)]
            for j in range(4):
                for k in range(31):
                    wcol = self.cv(f"cdw{l}", j * 31 + k)
                    self.dve(lambda e, j=j, k=k, wcol=wcol: e.tensor_scalar(
                        out=dg[:, j, k, :], in0=self.ident[:], scalar1=wcol, scalar2=None, op0=ALU.mult),
                        r=["ident", "cvec"], w=[("dg", j)])

            def conv_gen(ti):
                cs_ = self.nxt("cin", 2)
                ci = cin[cs_]
                c0 = ti * N
                self.dma(ci[:], CS[:, :, c0:c0 + N + 30], f"cin{cs_}",
                         r=[("CS", ti - 1), ("CS", ti), ("CS", ti + 1)], w=[("cin", cs_)])
                yield
                bkm = (banks[0], ("bank", 0))
                bkq = (banks[1], ("bank", 1))
                for j in range(4):
                    for k in range(31):
                        self.pe(lambda e, j=j, k=k: e.matmul(banks[6][:, :], lhsT=dg[:, j, k, :], rhs=ci[:, j, k:k + N],
                                                             start=(k == 0), stop=(k == 30)),
                                r=[("cin", cs_), ("dg", j)], w=[("bank", 6)])
                    bcol = self.cv(f"cdb{l}", j)
                    self.act(lambda e, j=j, bcol=bcol: e.activation(out=cc[:, j, :], in_=banks[6][:, :],
                                                                    func=AF.Identity, bias=bcol),
                             r=[("bank", 6), "cvec"], w=[("cc", j)])
                    yield
                for j in range(4):
                    self.ln_stats_chunk(L, cc[:, j, :], ("cc", j), j, 4, N, 512, bkm, bkq)
                self.ln_finalize(L, N, bkm, bkq)
                ns = self.nxt("cn", 2)
                for j in range(4):
                    self.ln_norm_chunk(L, cc[:, j, :], ("cc", j), j, N, f"clg{l}", f"clb{l}", AF.Silu,
                                       [(cn[ns][:, j, :], ("cn", ns, j))])
                self.dma(CN[:, :, c0:c0 + N], cn[ns][:], f"cn{ns}", r=[("cn", ns, j) for j in range(4)], q="act")
                yield

            def mm_tile(slot, bank, bkey, m):
                for k in range(8):
                    self.pe(lambda e, k=k: e.matmul(
                        bank[:, :], lhsT=win[:, k, m * 128:(m + 1) * 128], rhs=xbt[slot][:, k, :],
                        start=(k == 0), stop=(k == 7)),
                        r=[("xbt", slot)] + wkeys, w=[bkey])

            def load_x(it):
                i = T - 1 - it
                s = it % 2
                c0 = i * N
                self.dma(xbt[s][:], XB[:, :, 1 + c0:1 + c0 + N], f"xbt{s}", w=[("xbt", s)])

            def glu_gen(it):
                i = T - 1 - it
                s = it % 2
                c0 = i * N
                for j in range(4):
                    bv, bg = banks[0], banks[1]
                    kv, kg = ("bank", 0), ("bank", 1)
                    mm_tile(s, bg, kg, 4 + j)
                    mm_tile(s, bv, kv, j)
                    q = self.nxt("sg", 2)
                    self.act(lambda e, q=q: e.activation(out=sg[q][:], in_=bg[:, :], func=AF.Sigmoid),
                             r=[kg], w=[("sg", q)])
                    self.dve(lambda e, q=q, j=j: e.tensor_tensor(out=cst[s][:, j, :], in0=bv[:, :],
                                                                 in1=sg[q][:], op=ALU.mult),
                             r=[kv, ("sg", q)], w=[("cst", s, j)])
                    yield
                self.dma(CS[:, :, 15 + c0:15 + c0 + N], cst[s][:], f"cst{s}",
                         r=[("cst", s, j) for j in range(4)], w=[("CS", i)], q="act")
                yield

            load_x(0)
            for _ in glu_gen(0):
                pass
            for it in range(T + 1):
                i = T - 1 - it
                cg = conv_gen(i + 1) if it >= 1 else None
                if it < T:
                    s = it % 2
                    c0 = i * N
                    gg_ = None
                    if it + 1 < T:
                        load_x(it + 1)
                        gg_ = glu_gen(it + 1)
                    if cg is not None:
                        next(cg)

                    def rx_src(j, s=s):
                        b = 2 + (j % 2)
                        mm_tile(s, banks[b], ("bank", b), 8 + j)
                        return banks[b], ("bank", b)
                    rg = self.emit_rnn_gen(Rn, l, 1, it, it == 0, rx_src,
                                           [((banks[4], ("bank", 4)), (banks[5], ("bank", 5)))], reverse=True)
                    for k, _ in enumerate(rg):
                        if k in (1, 3, 5, 7) and cg is not None:
                            next(cg, None)
                        if k in (0, 2, 4, 6, 8) and gg_ is not None:
                            next(gg_, None)
                    if gg_ is not None:
                        for _ in gg_:
                            pass
                    if cg is not None:
                        for _ in cg:
                            pass
                    hs = Rn["h"][it % 2]
                    self.dma(HB[:, :, c0:c0 + N], hs[:], f"hs{it % 2}", r=[("h", it % 2, j) for j in range(4)],
                             q="act")
                else:
                    for _ in cg:
                        pass
            self.P.emit_phase(f"S1_{si}_{l}")

    def phase_s2a(self, si, l):
        nc = self.nc
        S = self.seq_lens[si]
        T = S // N
        XB = self.XB[si][l].rearrange("(k p) s -> p k s", p=128)
        XF = self.XF[si][l].rearrange("(k p) s -> p k s", p=128)
        CN = self.CN[si].rearrange("(k p) s -> p k s", p=128)
        HB = self.HB[si].rearrange("(k p) s -> p k s", p=128)
        X1F = self.X1F[si].rearrange("(k p) s -> p k s", p=128)
        X1B = self.X1B[si].rearrange("(k p) s -> p k s", p=128)
        with ExitStack() as es:
            sb = lambda name, shape, dtp: es.enter_context(nc.sbuf_tensor(self.un(name), shape, dtp))
            self.epsc = sb("epsc", [128, 1], F32)
            self.onec = sb("onec", [128, 1], F32)
            self.dve(lambda e: e.memset(self.epsc[:], EPS), w=["epsc"])
            self.dve(lambda e: e.memset(self.onec[:], 1.0), w=["onec"])
            win = sb("win", [128, 8, 1024], BF16)
            wout = sb("wout", [128, 8, 1024], BF16)
            xbt = [sb(f"xbt{i}", [128, 8, N], BF16) for i in range(2)]
            R = [sb(f"R{i}", [128, 8, N], F32) for i in range(2)]
            cn = [sb(f"cn{i}", [128, 4, N], BF16) for i in range(2)]
            hbin = [sb(f"hbin{i}", [128, 4, N], F32) for i in range(1)]
            Rn = self.rnn_alloc(es)
            gg = [sb(f"gg{i}", [128, N], F32) for i in range(2)]
            rec = [sb(f"rec{i}", [128, 4, N], BF16) for i in range(2)]
            rsum = [sb(f"rsum{i}", [128, N], F32) for i in range(1)]
            x1b = [sb(f"x1b{i}", [128, 8, N], BF16) for i in range(1)]
            L = self.ln_alloc(es, "s2a")
            banks = self.psum_banks(es)
            wsrc = self.win_b[l * 128:(l + 1) * 128, :].rearrange("p (k m) -> p k m", k=8)
            wosrc = self.wout_b[l * 128:(l + 1) * 128, :].rearrange("p (k m) -> p k m", k=8)
            for h in range(2):
                self.dma(win[:, 4 * h:4 * h + 4, :], wsrc[:, 4 * h:4 * h + 4, 1024:2048], f"w{h}", w=[("win", h)])
                self.dma(wout[:, 4 * h:4 * h + 4, :], wosrc[:, 4 * h:4 * h + 4, :], f"wo{h}", w=[("wout", h)])
            wkeys = [("win", 0), ("win", 1)]
            wokeys = [("wout", 0), ("wout", 1)]
            bkm = (banks[6], ("bank", 6))
            bkq = (banks[7], ("bank", 7))

            def loads_r(i):
                s = i % 2
                c0 = i * N
                self.dma(xbt[s][:], XB[:, :, 1 + c0:1 + c0 + N], f"xbt{s}", w=[("xbt", s)])
                self.dma(hbin[0][:], HB[:, :, c0:c0 + N], "hbin0", w=[("hbin", 0, j) for j in range(4)])

            def loads_o(i):
                s = i % 2
                c0 = i * N
                self.dma(cn[s][:], CN[:, :, c0:c0 + N], f"cn{s}", w=[("cn", s, j) for j in range(4)])
                self.dma(R[s][:], XF[:, :, 1 + c0:1 + c0 + N], f"R{s}", w=[("R", s, k) for k in range(8)])

            def stage_r(i):
                s = i % 2

                def mm(bank, bkey, m):
                    for k in range(8):
                        self.pe(lambda e, k=k: e.matmul(
                            bank[:, :], lhsT=win[:, k, m * 128:(m + 1) * 128], rhs=xbt[s][:, k, :],
                            start=(k == 0), stop=(k == 7)),
                            r=[("xbt", s)] + wkeys, w=[bkey])

                def rx_src(j):
                    b = 0 + (j % 2)
                    mm(banks[b], ("bank", b), j)
                    return banks[b], ("bank", b)
                rg = self.emit_rnn_gen(Rn, l, 0, i, i == 0, rx_src,
                                       [((banks[2], ("bank", 2)), (banks[3], ("bank", 3)))], reverse=False)
                hs = Rn["h"][i % 2]

                def gate_chunk(j):
                    b = 0 + (j % 2)
                    mm(banks[b], ("bank", b), 4 + j)
                    q = self.nxt("gg", 2)
                    self.act(lambda e: e.activation(out=gg[q][:], in_=banks[b][:, :], func=AF.Gelu_apprx_tanh),
                             r=[("bank", b)], w=[("gg", q)])
                    return q

                def rec_chunk(j, q):
                    self.pool(lambda e: e.tensor_tensor(out=rsum[0][:], in0=hs[:, j, :], in1=hbin[0][:, j, :],
                                                        op=ALU.add),
                              r=[("h", i % 2, j), ("hbin", 0, j)], w=[("rsum", 0)])
                    self.dve(lambda e: e.tensor_tensor(out=rec[s][:, j, :], in0=rsum[0][:], in1=gg[q][:],
                                                       op=ALU.mult),
                             r=[("rsum", 0), ("gg", q)], w=[("rec", s, j)])
                for j in range(4):
                    next(rg)
                    yield
                qs = []
                for st in range(7):
                    next(rg)
                    if st >= 3:
                        rec_chunk(st - 3, qs[st - 3])
                    if 2 <= st <= 5:
                        qs.append(gate_chunk(st - 2))
                    if st == 6 and i + 1 < T:
                        loads_r(i + 1)
                    yield
                for _ in rg:
                    pass

            def stage_o(i):
                s = i % 2
                c0 = i * N
                pend = []
                for m in range(8):
                    b = 4 + (m % 2)
                    for k in range(8):
                        rhs = cn[s][:, k, :] if k < 4 else rec[s][:, k - 4, :]
                        rkey = ("cn", s, k) if k < 4 else ("rec", s, k - 4)
                        self.pe(lambda e, k=k, rhs=rhs, b=b, m=m: e.matmul(
                            banks[b][:, :], lhsT=wout[:, k, m * 128:(m + 1) * 128], rhs=rhs,
                            start=(k == 0), stop=(k == 7)),
                            r=[rkey] + wokeys, w=[("bank", b)])
                    self.dve(lambda e, b=b, m=m: e.scalar_tensor_tensor(
                        out=R[s][:, m, :], in0=R[s][:, m, :], scalar=ALPHA, in1=banks[b][:, :], op0=ALU.mult,
                        op1=ALU.add),
                        r=[("R", s, m), ("bank", b)], w=[("R", s, m)])
                    if m >= 1:
                        self.ln_stats_pe(L, pend.pop(0), m - 1, 8, N, 1024, bkm, bkq)
                    pend.append(self.ln_stats_act(L, R[s][:, m, :], ("R", s, m), N, 3))
                    yield
                self.ln_stats_pe(L, pend.pop(0), 7, 8, N, 1024, bkm, bkq)
                self.ln_finalize(L, N, bkm, bkq)
                yield
                for k in range(8):
                    outs_k = [(R[s][:, k, :], ("R", s, k)), (x1b[0][:, k, :], ("x1b", 0, k))]
                    self.ln_norm_chunk(L, R[s][:, k, :], ("R", s, k), k, N, f"l1g{l}", f"l1b{l}", AF.Identity, outs_k)
                    if k == 3:
                        yield
                self.dma(X1F[:, :, 1 + c0:1 + c0 + N], R[s][:], f"R{s}", r=[("R", s, k) for k in range(8)], q="act")
                self.dma(X1B[:, :, c0:c0 + N], x1b[0][:], "x1b0", r=[("x1b", 0, k) for k in range(8)], q="act")
                yield

            loads_r(0)
            prev = None
            for i in range(T):
                loads_o(i)
                for _ in stage_r(i):
                    if prev is not None:
                        next(prev, None)
                if prev is not None:
                    for _ in prev:
                        pass
                prev = stage_o(i)
            for _ in prev:
                pass
            self.P.emit_phase(f"S2a_{si}_{l}")

    def phase_s2b(self, si, l):
        nc = self.nc
        S = self.seq_lens[si]
        T = S // N
        X1F = self.X1F[si].rearrange("(k p) s -> p k s", p=128)
        X1B = self.X1B[si].rearrange("(k p) s -> p k s", p=128)
        XFo = self.XF[si][l + 1].rearrange("(k p) s -> p k s", p=128)
        XBo = self.XB[si][l + 1].rearrange("(k p) s -> p k s", p=128) if l == 0 else None
        with ExitStack() as es:
            sb = lambda name, shape, dtp: es.enter_context(nc.sbuf_tensor(self.un(name), shape, dtp))
            self.epsc = sb("epsc", [128, 1], F32)
            self.dve(lambda e: e.memset(self.epsc[:], EPS), w=["epsc"])
            x1t = [sb(f"x1t{i}", [128, 8, N], BF16) for i in range(2)]
            R = [sb(f"R{i}", [128, 8, N], F32) for i in range(2)]
            wup = [sb(f"wup{i}", [128, 8, 512], BF16) for i in range(3)]
            wdn = [sb(f"wdn{i}", [128, 22, 128], BF16) for i in range(3)]
            ub4 = [sb(f"ub4{i}", [128, 4, N + 2], F32) for i in range(2)]
            uc4 = [sb(f"uc4{i}", [128, 4, N], F32) for i in range(2)]
            gg = [sb(f"gg{i}", [128, 2, N], F32) for i in range(1)]
            gbuf = [sb(f"gbuf{i}", [128, 22, N], BF16) for i in range(2)]
            x2b = sb("x2b", [128, 8, N], BF16)
            uhist = sb("uhist", [128, NUC, 2], F32)
            L = self.ln_alloc(es, "s2b")
            pp = [es.enter_context(nc.psum_tensor(self.un(f"pp{i}"), [128, 2, 512], F32)) for i in range(2)]
            banks = self.psum_banks(es, 4)
            self.pool(lambda e: e.memset(uhist[:], 0.0), w=["uhist"])

            def conv_piece(pj, n, us, flush):
                ub = ub4[us]
                uc = uc4[us]
                for qq in range(4):
                    c = pj * 4 + qq
                    w0 = self.cv(f"fdw{l}", c * 3 + 0)
                    w1 = self.cv(f"fdw{l}", c * 3 + 1)
                    w2 = self.cv(f"fdw{l}", c * 3 + 2)
                    bc = self.cv(f"fdb{l}", c)
                    hk = ("ub4", us, qq // 2)
                    self.pool(lambda e, qq=qq, w0=w0, bc=bc: e.tensor_scalar(
                        out=uc[:, qq, :n], in0=ub[:, qq, 0:n], scalar1=w0, scalar2=bc, op0=ALU.mult, op1=ALU.add),
                        r=[hk, ("ub4h", us), "cvec"], w=[("uc4", us, qq)])
                    self.dve(lambda e, qq=qq, w1=w1: e.scalar_tensor_tensor(
                        out=uc[:, qq, :n], in0=ub[:, qq, 1:n + 1], scalar=w1, in1=uc[:, qq, :n],
                        op0=ALU.mult, op1=ALU.add),
                        r=[hk, ("ub4h", us), ("uc4", us, qq)], w=[("uc4", us, qq)])
                    self.dve(lambda e, qq=qq, w2=w2: e.scalar_tensor_tensor(
                        out=uc[:, qq, :n], in0=ub[:, qq, 2:n + 2], scalar=w2, in1=uc[:, qq, :n],
                        op0=ALU.mult, op1=ALU.add),
                        r=[hk, ("uc4", us, qq)], w=[("uc4", us, qq)])
                return uc

            def gate_piece(pj, n, us, gs):
                uc = uc4[us]
                q = 0
                self.act(lambda e: e.activation(out=gg[q][:, :, :n], in_=uc[:, 2:4, :n], func=AF.Gelu_apprx_tanh),
                         r=[("uc4", us, 2), ("uc4", us, 3)], w=[("gg", q)])
                self.dve(lambda e: e.tensor_tensor(out=gbuf[gs][:, 2 * pj:2 * pj + 2, :n], in0=gg[q][:, :, :n],
                                                   in1=uc[:, 0:2, :n], op=ALU.mult),
                          r=[("gg", q), ("uc4", us, 0), ("uc4", us, 1)], w=[("gbuf", gs, pj)])

            def stage_a(i):
                s = i % 2
                gs = i % 2
                c0 = i * N
                if i == 0:
                    self.dma(x1t[s][:], X1B[:, :, c0:c0 + N], f"x1t{s}", w=[("x1t", s)])
                for pj in range(NPC):
                    if pj == 4:
                        self.dma(R[s][:], X1F[:, :, c0:c0 + N], f"R{s}", w=[("R", s, k) for k in range(8)])
                    if pj == 6 and i + 1 < T:
                        s2 = (i + 1) % 2
                        c2 = (i + 1) * N
                        self.dma(x1t[s2][:], X1B[:, :, c2:c2 + N], f"x1t{s2}", w=[("x1t", s2)])
                    ws = self.nxt("wup", 3)
                    row = (l * NPC + pj) * 128
                    self.dma(wup[ws][:], self.wup_b[row:row + 128, :].rearrange("p (k c) -> p k c", k=8),
                             f"wup{ws}", w=[("wup", ws)])
                    us = self.nxt("ub4", 2)
                    ub = ub4[us]
                    self.pool(lambda e, ub=ub, pj=pj: e.tensor_copy(out=ub[:, :, 0:2], in_=uhist[:, 4 * pj:4 * pj + 4, :]),
                              r=["uhist"], w=[("ub4h", us)])
                    for u in range(2):
                        pb = self.nxt("pp", 2)
                        for q2 in range(2):
                            qq = 2 * u + q2
                            for k in range(8):
                                self.pe(lambda e, k=k, pb=pb, q2=q2, ws=ws, qq=qq, s=s: e.matmul(
                                    pp[pb][:, q2, :], lhsT=wup[ws][:, k, qq * 128:(qq + 1) * 128], rhs=x1t[s][:, k, :],
                                    start=(k == 0), stop=(k == 7)),
                                    r=[("x1t", s), ("wup", ws)], w=[("pp", pb)])
                        self.act(lambda e, ub=ub, u=u, pb=pb: e.activation(
                            out=ub[:, 2 * u:2 * u + 2, 2:N + 2], in_=pp[pb][:, :, :], func=AF.Copy),
                            r=[("pp", pb)], w=[("ub4", us, u)])
                    conv_piece(pj, N, us, False)
                    self.pool(lambda e, ub=ub, pj=pj: e.tensor_copy(out=uhist[:, 4 * pj:4 * pj + 4, :], in_=ub[:, :, N:N + 2]),
                              r=[("ub4", us, 0), ("ub4", us, 1)], w=["uhist"])
                    gate_piece(pj, N, us, gs)
                    yield

            def stage_bc(i, n, flush):
                rs = i % 2
                gs = i % 2
                bkm = (banks[2], ("bank", 2))
                bkq = (banks[3], ("bank", 3))
                pend = []
                for m in range(8):
                    ws = self.nxt("wdn", 3)
                    row = (l * 8 + m) * 128
                    self.dma(wdn[ws][:], self.wdn_b[row:row + 128, :].rearrange("p (k c) -> p k c", k=22),
                             f"wdn{ws}", w=[("wdn", ws)])
                    b = m % 2
                    for k in range(22):
                        self.pe(lambda e, k=k, b=b, ws=ws: e.matmul(
                            banks[b][:, :n], lhsT=wdn[ws][:, k, :], rhs=gbuf[gs][:, k, :n],
                            start=(k == 0), stop=(k == 21)),
                            r=[("gbuf", gs, k // 2), ("wdn", ws)], w=[("bank", b)])
                    self.dve(lambda e, b=b, m=m: e.scalar_tensor_tensor(
                        out=R[rs][:, m, :n], in0=R[rs][:, m, :n], scalar=ALPHA, in1=banks[b][:, :n],
                        op0=ALU.mult, op1=ALU.add),
                        r=[("R", rs, m), ("bank", b)], w=[("R", rs, m)])
                    if m >= 2:
                        self.ln_stats_pe(L, pend.pop(0), m - 2, 8, n, 1024, bkm, bkq)
                    if m >= 1:
                        pend.append(self.ln_stats_act(L, R[rs][:, m - 1, :n], ("R", rs, m - 1), n, 3))
                    yield
                self.ln_stats_pe(L, pend.pop(0), 6, 8, n, 1024, bkm, bkq)
                pend.append(self.ln_stats_act(L, R[rs][:, 7, :n], ("R", rs, 7), n, 3))
                yield
                self.ln_stats_pe(L, pend.pop(0), 7, 8, n, 1024, bkm, bkq)
                self.ln_finalize(L, n, bkm, bkq)
                for k in range(8):
                    outs_k = [(R[rs][:, k, :n], ("R", rs, k))] + ([(x2b[:, k, :n], ("x2b", k))] if l == 0 else [])
                    self.ln_norm_chunk(L, R[rs][:, k, :n], ("R", rs, k), k, n, f"l2g{l}", f"l2b{l}", AF.Identity, outs_k)
                    if k == 3:
                        yield
                c0 = S if flush else i * N
                self.dma(XFo[:, :, c0:c0 + n], R[rs][:, :, :n], f"R{rs}", r=[("R", rs, k) for k in range(8)],
                         slow=flush, q="act")
                if l == 0:
                    self.dma(XBo[:, :, c0:c0 + n], x2b[:, :, :n], "x2b", r=[("x2b", k) for k in range(8)],
                             slow=flush, q="act")
                yield

            prev = None
            for i in range(T):
                ga = stage_a(i)
                for _ in ga:
                    if prev is not None:
                        next(prev, None)
                if prev is not None:
                    for _ in prev:
                        pass
                prev = stage_bc(i, N, False)
            rs = T % 2
            gs = T % 2
            self.dma(R[rs][:, :, 0:1], X1F[:, :, S:S + 1], f"R{rs}", w=[("R", rs, k) for k in range(8)], slow=True)
            for pj in range(NPC):
                us = self.nxt("ub4", 2)
                ub = ub4[us]
                self.pool(lambda e, ub=ub, pj=pj: e.tensor_copy(out=ub[:, :, 0:2], in_=uhist[:, 4 * pj:4 * pj + 4, :]),
                          r=["uhist"], w=[("ub4h", us)])
                self.pool(lambda e, ub=ub: e.memset(ub[:, :, 2:3], 0.0), w=[("ub4", us, 0), ("ub4", us, 1)])
                conv_piece(pj, 1, us, True)
                gate_piece(pj, 1, us, gs)
                next(prev, None)
            for _ in prev:
                pass
            for _ in stage_bc(T, 1, True):
                pass
            self.P.emit_phase(f"S2b_{si}_{l}")

    def phase_out(self, si):
        nc = self.nc
        S = self.seq_lens[si]
        T = S // N
        XF = self.XF[si][2].rearrange("(k p) s -> p k s", p=128)
        y = self.y_out[si]
        with ExitStack() as es:
            sb = lambda name, shape, dtp: es.enter_context(nc.sbuf_tensor(self.un(name), shape, dtp))
            R = [sb(f"R{i}", [128, 8, N], F32) for i in range(2)]
            ot = [sb(f"ot{i}", [128, D], F32) for i in range(3)]
            pt = [es.enter_context(nc.psum_tensor(self.un(f"pt{i}"), [128, 1024], F32)) for i in range(2)]
            for i in range(T):
                rs = i % 2
                c0 = 1 + i * N
                self.dma(R[rs][:], XF[:, :, c0:c0 + N], f"R{rs}", w=[("R", rs)])
                for tb in range(4):
                    ps = self.nxt("pt", 2)
                    os_ = self.nxt("ot", 3)
                    for k in range(8):
                        self.pe(lambda e, rs=rs, ps=ps, k=k, tb=tb: e.transpose(
                            out=pt[ps][:, k * 128:(k + 1) * 128], in_=R[rs][:, k, tb * 128:(tb + 1) * 128],
                            identity=self.ident[:]),
                            r=[("R", rs), "ident"], w=[("pt", ps)])
                    for hh in range(2):
                        fn = (lambda e, os_=os_, ps=ps, hh=hh: e.activation(
                            out=ot[os_][:, hh * 512:(hh + 1) * 512], in_=pt[ps][:, hh * 512:(hh + 1) * 512],
                            func=AF.Copy))
                        if hh == 0:
                            self.act(fn, r=[("pt", ps)], w=[("ot", os_, hh)])
                        else:
                            fn2 = (lambda e, os_=os_, ps=ps, hh=hh: e.tensor_copy(
                                out=ot[os_][:, hh * 512:(hh + 1) * 512], in_=pt[ps][:, hh * 512:(hh + 1) * 512]))
                            self.dve(fn2, r=[("pt", ps)], w=[("ot", os_, hh)])
                    row0 = i * N + tb * 128
                    self.dma(y[row0:row0 + 128, :], ot[os_][:], f"ot{os_}",
                             r=[("ot", os_, 0), ("ot", os_, 1)])
            self.P.emit_phase(f"PO_{si}")


_NC_CACHE = {}


def _get_nc(seq_lens):
    key = tuple(seq_lens)
    if key not in _NC_CACHE:
        nc = bass.Bass("TRN2", target_bir_lowering=False)
        b = Builder(nc, list(seq_lens))
        b.build()
        _NC_CACHE[key] = nc
    return _NC_CACHE[key]


def run_cores(xa_list, xb_list, inp):
    SA = xa_list[0].shape[0]
    SB = xb_list[0].shape[0]
    nc = _get_nc((SA, SB))
    cvec = _pack_cvec(inp)
    wts = _pack_weights(inp)
    in_maps = []
    for a, b in zip(xa_list, xb_list):
        m = {"x0": np.ascontiguousarray(a, np.float32), "x1": np.ascontiguousarray(b, np.float32), "cvec": cvec}
        m.update(wts)
        in_maps.append(m)
    res = run_bass_kernel_spmd(nc, in_maps, core_ids=list(range(len(in_maps))))
    return [(r["y0"], r["y1"]) for r in res.results]


def kernel(**inputs):
    xp = np.asarray(inputs["x_prompt"], np.float32)
    xs = np.asarray(inputs["x_sample"], np.float32)
    xa = [xs[c] for c in range(8)]
    xb = [xp[c % 4] for c in range(8)]
    outs = run_cores(xa, xb, inputs)
    y_sample = np.stack([outs[c][0] for c in range(8)], axis=0)
    y_prompt = np.stack([outs[c][1] for c in range(4)], axis=0)
    return (y_prompt.astype(np.float32), y_sample.astype(np.float32))
```
